# Optimizing a Trainium2 kernel written in Bass

```python
import jax, jax.numpy as jnp
from jax import lax
import numpy as np

D_MODEL = 1024
BATCH = 2
SEQ = 8192
DEPTH = 1

D_MIX = 1024
ATTN_WIDTH = 512
N_Q_HEADS = 8
N_KV_HEADS = 2
HEAD_DIM = 64
KV_WIDTH = N_KV_HEADS * HEAD_DIM
WINDOW = 128
BLOCK = 128
ROPE_THETA = 10000.0
POOL_WIDTH = 512
POOL_WINDOWS = (2, 4, 8, 16)
N_POOL_GROUPS = 4
POOL_GROUP_DIM = 128
RMS_EPS = 1e-5
SPLIT_SIZES = (ATTN_WIDTH, KV_WIDTH, KV_WIDTH, ATTN_WIDTH, POOL_WIDTH, POOL_WIDTH)
D_IN = sum(SPLIT_SIZES)

kernel_name = "hymba_swa_sink_multiscale_pool_block"


def rmsnorm(x, g):
    xf = x.astype(jnp.float32)
    y = xf * lax.rsqrt(jnp.mean(xf * xf, axis=-1, keepdims=True) + RMS_EPS)
    return (y * g.astype(jnp.float32)).astype(x.dtype)


def apply_rope(x, pos):
    half = HEAD_DIM // 2
    inv_freq = 1.0 / (ROPE_THETA ** (jnp.arange(half, dtype=jnp.float32) * (2.0 / HEAD_DIM)))
    ang = pos.astype(jnp.float32)[:, None] * inv_freq[None, :]
    cos = jnp.cos(ang)[None, :, None, :]
    sin = jnp.sin(ang)[None, :, None, :]
    xf = x.astype(jnp.float32)
    x1, x2 = xf[..., :half], xf[..., half:]
    out = jnp.concatenate([x1 * cos - x2 * sin, x2 * cos + x1 * sin], axis=-1)
    return out.astype(x.dtype)


def sliding_window_attention_with_sinks(q, k, v, sinks):
    B, S = q.shape[0], q.shape[1]
    nb = S // BLOCK
    G = N_Q_HEADS // N_KV_HEADS
    qb = q.reshape(B, nb, BLOCK, N_KV_HEADS, G, HEAD_DIM)

    def band(t):
        tb = t.reshape(B, nb, BLOCK, N_KV_HEADS, HEAD_DIM)
        prev = jnp.pad(tb, ((0, 0), (1, 0), (0, 0), (0, 0), (0, 0)))[:, :-1]
        return jnp.concatenate([prev, tb], axis=2)

    kw, vw = band(k), band(v)
    scores = jnp.einsum('bnqkgd,bnskd->bnkgqs', qb, kw).astype(jnp.float32) * (HEAD_DIM ** -0.5)

    blk = jnp.arange(nb)[:, None]
    qpos = blk * BLOCK + jnp.arange(BLOCK)[None, :]
    kpos = (blk - 1) * BLOCK + jnp.arange(2 * BLOCK)[None, :]
    diff = qpos[:, :, None] - kpos[:, None, :]
    mask = (diff >= 0) & (diff < WINDOW) & (kpos[:, None, :] >= 0)
    scores = jnp.where(mask[None, :, None, None], scores, jnp.float32(-1e30))

    sink = sinks.astype(jnp.float32).reshape(N_KV_HEADS, G)[None, None, :, :, None, None]
    m = jnp.maximum(jnp.max(scores, axis=-1, keepdims=True), sink)
    p = jnp.exp(scores - m)
    denom = jnp.sum(p, axis=-1, keepdims=True) + jnp.exp(sink - m)
    probs = (p / denom).astype(v.dtype)
    out = jnp.einsum('bnkgqs,bnskd->bnqkgd', probs, vw)
    return out.reshape(B, S, N_Q_HEADS * HEAD_DIM)


def multiscale_causal_pool(u):
    B, S = u.shape[0], u.shape[1]
    uf = u.astype(jnp.float32)
    c = jnp.pad(jnp.cumsum(uf, axis=1), ((0, 0), (1, 0), (0, 0)))
    t = jnp.arange(S)
    outs = []
    for g, w in enumerate(POOL_WINDOWS):
        cg = c[..., g * POOL_GROUP_DIM:(g + 1) * POOL_GROUP_DIM]
        lo = jnp.maximum(t + 1 - w, 0)
        window_sum = cg[:, 1:] - jnp.take(cg, lo, axis=1)
        count = jnp.minimum(t + 1, w).astype(jnp.float32)
        outs.append(window_sum / count[None, :, None])
    mean = jnp.concatenate(outs, axis=-1)
    return (mean - uf).astype(u.dtype)


def hybrid_layer(x, norm_g, w_in, sinks, w_pool, pool_scale, w_out):
    B, S = x.shape[0], x.shape[1]
    h = rmsnorm(x, norm_g)
    proj = jnp.einsum('bsd,de->bse', h, w_in)
    offs = np.cumsum((0,) + SPLIT_SIZES)
    q, k, v, gate_a, u_p, gate_p = [proj[..., offs[i]:offs[i + 1]] for i in range(len(SPLIT_SIZES))]

    pos = jnp.arange(S)
    q = apply_rope(q.reshape(B, S, N_Q_HEADS, HEAD_DIM), pos)
    k = apply_rope(k.reshape(B, S, N_KV_HEADS, HEAD_DIM), pos)
    v = v.reshape(B, S, N_KV_HEADS, HEAD_DIM)
    a = sliding_window_attention_with_sinks(q, k, v, sinks)
    a = a * jax.nn.silu(gate_a)

    pm = multiscale_causal_pool(u_p).reshape(B, S, N_POOL_GROUPS, POOL_GROUP_DIM)
    pm = jnp.einsum('bsgc,gcd->bsgd', pm, w_pool).reshape(B, S, POOL_WIDTH)
    pm = pm * pool_scale * jax.nn.silu(gate_p)

    mixed = jnp.concatenate([a, pm], axis=-1)
    return x + jnp.einsum('bse,ed->bsd', mixed, w_out)


def setup_inputs(seed: int = 0) -> dict:
    key = jax.random.key(seed)
    ks = jax.random.split(key, 8)
    x = jax.random.normal(ks[0], (BATCH, SEQ, D_MODEL), jnp.float32)
    norm_in = 1.0 + 0.02 * jax.random.normal(ks[1], (DEPTH, D_MODEL), jnp.float32)
    w_in = jax.random.normal(ks[2], (DEPTH, D_MODEL, D_IN), jnp.float32) * (D_MODEL ** -0.5)
    attn_sinks = 0.5 * jax.random.normal(ks[3], (DEPTH, N_Q_HEADS), jnp.float32)
    w_pool = jax.random.normal(ks[4], (DEPTH, N_POOL_GROUPS, POOL_GROUP_DIM, POOL_GROUP_DIM), jnp.float32) * (POOL_GROUP_DIM ** -0.5)
    pool_scale = 1.0 + 0.02 * jax.random.normal(ks[5], (DEPTH, POOL_WIDTH), jnp.float32)
    w_out = jax.random.normal(ks[6], (DEPTH, D_MIX, D_MODEL), jnp.float32) * (D_MIX ** -0.5)
    norm_final = 1.0 + 0.02 * jax.random.normal(ks[7], (D_MODEL,), jnp.float32)
    return {"x": x, "norm_in": norm_in, "w_in": w_in, "attn_sinks": attn_sinks,
            "w_pool": w_pool, "pool_scale": pool_scale, "w_out": w_out,
            "norm_final": norm_final}


def reference(x, norm_in, w_in, attn_sinks, w_pool, pool_scale, w_out, norm_final):
    h = x
    for l in range(DEPTH):
        h = hybrid_layer(h, norm_in[l], w_in[l], attn_sinks[l], w_pool[l], pool_scale[l], w_out[l])
    return rmsnorm(h, norm_final)
```

```python
import math
from contextlib import ExitStack

import numpy as np
import concourse.bass as bass
import concourse.mybir as mybir
from concourse.bass_utils import run_bass_kernel_spmd

F32 = mybir.dt.float32
BF16 = mybir.dt.bfloat16
ALU = mybir.AluOpType
AF = mybir.ActivationFunctionType

N_CORES = 8
D = 1024
SEQ = 8192
BATCH = 2
T = 2048
NT = T // 128
NG = 4
DIN = 2304
C_Q, C_K, C_V, C_U, C_GA, C_GP = 0, 512, 640, 768, 1280, 1792
EPS = 1e-5
XSLOTS = 8
VW = 80
WINDOWS = (2, 4, 8, 16)


class Op:
    __slots__ = ("eng", "fn", "deps", "dma", "sig", "idx")


class Sched:
    def __init__(self):
        self.ops = []
        self.last_w = {}
        self.readers = {}

    def add(self, eng, fn, reads=(), writes=(), dma=None):
        op = Op()
        op.eng, op.fn, op.dma, op.sig, op.idx = eng, fn, dma, None, len(self.ops)
        deps = set()
        for r in reads:
            if r not in self.last_w:
                raise RuntimeError("read of never-written resource %r" % (r,))
            deps.add(self.last_w[r])
        for w in writes:
            if w in self.last_w:
                deps.add(self.last_w[w])
            for rd in self.readers.get(w, ()):
                deps.add(rd)
        for r in reads:
            self.readers.setdefault(r, []).append(op.idx)
        for w in writes:
            self.last_w[w] = op.idx
            self.readers[w] = []
        op.deps = sorted(deps)
        self.ops.append(op)
        return op.idx

    def emit(self, nc, es):
        ops = self.ops
        needed = [False] * len(ops)
        for op in ops:
            for d in op.deps:
                needed[d] = True
        cnt = {}
        sem_keys = []
        for op in ops:
            if op.dma is not None:
                k = ("dma", op.dma)
            elif needed[op.idx]:
                k = ("eng", op.eng)
            else:
                continue
            if k not in cnt:
                cnt[k] = 0
                sem_keys.append(k)
            cnt[k] += 16 if op.dma is not None else 1
            op.sig = (k, cnt[k])
        sems = {}
        for i, k in enumerate(sem_keys):
            sems[k] = es.enter_context(nc.semaphore("s%d" % i))
        block = es.enter_context(nc.Block())
        by_eng = {}
        for op in ops:
            by_eng.setdefault(op.eng, []).append(op)

        def run(eng_name, eng):
            known = {}
            for op in by_eng.get(eng_name, []):
                want = {}
                for d in op.deps:
                    dop = ops[d]
                    if dop.dma is None and dop.eng == eng_name and eng_name == "pe":
                        continue
                    k, v = dop.sig
                    if v > want.get(k, 0):
                        want[k] = v
                for k, v in want.items():
                    if known.get(k, 0) >= v:
                        continue
                    eng.wait_ge(sems[k], v)
                    known[k] = v
                ins = op.fn(eng)
                if op.sig is not None:
                    k, v = op.sig
                    ins.then_inc(sems[k], 16 if op.dma is not None else 1)

        @block.sync
        def _(e):
            run("sp", e)

        @block.scalar
        def _(e):
            run("act", e)

        @block.vector
        def _(e):
            run("dve", e)

        @block.gpsimd
        def _(e):
            run("pool", e)

        @block.tensor
        def _(e):
            run("pe", e)


def build_nc(debug=False):
    nc = bass.Bass("TRN2", target_bir_lowering=False)

    def din(name, shape):
        return nc.dram_tensor(name, list(shape), F32, kind="ExternalInput").ap()

    x_d = din("x", [T, D])
    xh_d = din("xh", [128, D])
    win_d = din("w_in", [D, DIN])
    wout_d = din("w_out", [D, D])
    wpool_d = din("w_pool", [4, 128, 128])
    g_d = din("g_row", [1, D]).partition_broadcast(128)
    gf_d = din("gf_row", [1, D]).partition_broadcast(128)
    sink_d = din("sink_rep", [128, 8])
    psc_d = din("pscale", [128, 4])
    cos_d = din("cos_t", [128, (NT + 1) * 32])
    sin_d = din("sin_t", [128, (NT + 1) * 32])
    atab_d = din("atab", [128, 3, 2, 128])
    btab_d = din("btab", [128, 64])
    band_d = din("band", [128, 12, 128])
    ident_d = din("ident", [128, 128])
    out_d = nc.dram_tensor("out", [T, D], F32, kind="ExternalOutput").ap()

    S = Sched()
    with ExitStack() as es:
        E = es.enter_context

        def sb(name, shape, dt):
            return E(nc.sbuf_tensor("sb_" + name, list(shape), dt))

        XS, XR = 4, 4
        xs = sb("xs", [128, XS, D], F32)
        xr = sb("xr", [128, XR, D], F32)
        junk = sb("junk", [128, D], BF16)
        hbf = sb("hbf", [128, 2, D], BF16)
        hT = sb("hT", [128, 2, 8, 512], BF16)
        hTh = sb("hTh", [128, 8, 128], BF16)
        winb = sb("winb", [128, 8, DIN], BF16)
        woutb = sb("woutb", [128, 8, D], BF16)
        wpoolb = sb("wpoolb", [128, 4, 128], BF16)
        qk_sb = sb("qk_sb", [128, 640], F32)
        ropeA = sb("ropeA", [128, 640], F32)
        ropeB = sb("ropeB", [128, 640], F32)
        qrot = sb("qrot", [128, 2, 640], BF16)
        qT = sb("qT", [128, 2, 4, 512], BF16)
        kT = sb("kT", [128, (NT + 1) * 128], BF16)
        vaug = sb("vaug", [128, NT + 1, 2, VW], BF16)
        tsb = sb("tsb", [128, 2, 512], F32)
        sga = sb("sga", [128, 4, 512], BF16)
        sgp = sb("sgp", [128, 4, 512], BF16)
        u_tm = sb("u_tm", [128, 3, 512], BF16)
        bandb = sb("bandb", [128, 12, 128], BF16)
        dT = sb("dT", [128, 2, 4, 512], BF16)
        PT = sb("PT", [128, 2, 4, 512], BF16)
        a_tm = sb("a_tm", [128, 2, 512], BF16)
        den = sb("den", [128, 2, 8], F32)
        mixT = sb("mixT", [128, 2, 8, 512], BF16)
        cs = sb("cs", [128, NT + 1, 32], F32)
        sn = sb("sn", [128, NT + 1, 32], F32)
        nsn = sb("nsn", [128, NT + 1, 32], F32)
        atab = sb("atab", [128, 3, 2, 128], BF16)
        btab = sb("btab", [128, 64], BF16)
        g32 = sb("g32", [128, D], F32)
        gfr = sb("gfr", [128, D], F32)
        identb = sb("identb", [128, 128], BF16)
        esink2 = sb("esink2", [128, 8], F32)
        psc05 = sb("psc05", [128, 4], F32)
        stat = sb("stat", [128, 64], F32)

        ps = [E(nc.psum_tensor("ps%d" % i, [128, 512], F32)) for i in range(8)]
        rot = {"i": 0}

        def nbank():
            b = rot["i"] % 8
            rot["i"] += 1
            return b

        def PSB(i):
            return ("psum", i)

        def psbf(i):
            return ps[i][:].bitcast(BF16)

        def dma(eng, key, out, in_, reads=(), writes=()):
            S.add(eng, lambda e: e.dma_start(out=out, in_=in_), reads=reads, writes=writes, dma=key)

        def load_xs(ti):
            sl = (ti + 1) % XS
            src = xh_d if ti < 0 else x_d[ti * 128:(ti + 1) * 128, :]
            dma("sp", ("xs", sl), xs[:, sl, :], src, writes=[("xs", sl)])

        def load_xr(ti):
            sl = ti % XR
            dma("sp", ("xr", sl), xr[:, sl, :], x_d[ti * 128:(ti + 1) * 128, :],
                writes=[("xr", sl, 0), ("xr", sl, 1)])

        load_xs(-1)
        dma("sp", "g", g32[:], g_d, writes=["g32"])
        load_xs(0)
        dma("sp", "cos", cs[:], cos_d.rearrange("p (t d) -> p t d", d=32), writes=["cs"])
        dma("sp", "sin", sn[:], sin_d.rearrange("p (t d) -> p t d", d=32), writes=["sn"])
        load_xs(1)
        load_xs(2)
        win_v = win_d.rearrange("(c p) n -> p c n", p=128)
        wout_v = wout_d.rearrange("(c p) n -> p c n", p=128)

        def wkeys(tag):
            return [(tag, c) for c in range(8)]

        def wdma(tag, lo, hi):
            dma("pool", (tag, 0), winb[:, :, lo:hi], win_v[:, :, lo:hi], writes=[(tag, c) for c in range(8)])
        W_KV, W_Q, W_U, W_GA, W_GP = (wkeys(t_) for t_ in ("wkv", "wq", "wu", "wga", "wgp"))
        WOUT = [[("wout", c, n) for c in range(8)] for n in range(2)]
        WINB = [W_GA, W_GP]

        def dma_wpool():
            dma("pool", "wpool", wpoolb[:], wpool_d.rearrange("g c d -> c g d"), writes=["wpoolb"])

        def dma_wout(c0, c1):
            for c in range(c0, c1):
                dma("pool", ("wout", c), woutb[:, c, :], wout_v[:, c, :], writes=[("wout", c, 0), ("wout", c, 1)])

        def dma_tables():
            dma("pool", "atab", atab[:], atab_d, writes=["atab"])
            dma("pool", "btab", btab[:], btab_d, writes=["btab"])
            dma("pool", "band0", bandb[:, 0:8, :], band_d[:, 0:8, :], writes=["bandb"])
            dma("pool", "band1", bandb[:, 8:12, :], band_d[:, 8:12, :], writes=["bandb"])

        dma("pool", "ident", identb[:], ident_d, writes=["identb"])
        wdma("wkv", C_K, C_K + 256)

        dma("sp", "sink", esink2[:], sink_d, writes=["esink2"])
        dma("sp", "psc", psc05[:], psc_d, writes=["psc05"])

        S.add("dve", lambda e: e.memset(stat[:, 0:1], -0.5), writes=["c_m05"])
        S.add("act", lambda e: e.activation(out=stat[:, 62:63], in_=stat[:, 0:1], func=AF.Exp),
              reads=["c_m05"], writes=[("stat", 62)])

        def setup_rope():
            S.add("dve", lambda e: e.tensor_scalar(out=nsn[:], in0=sn[:], scalar1=-1.0, scalar2=None, op0=ALU.mult),
                  reads=["sn"], writes=["nsn"])
            S.add("pool", lambda e: e.memset(vaug[:], 1.0), writes=[("vaug", t) for t in range(NT + 1)])

        def setup_late():
            S.add("dve", lambda e: e.tensor_scalar(out=psc05[:], in0=psc05[:], scalar1=0.5, scalar2=None,
                                                   op0=ALU.mult), writes=["psc05"])

        def hT_dst(ti):
            if ti < 0:
                return hTh[:, :, :], ("hTh",)
            gb, k = (ti // 4) % 2, ti % 4
            return hT[:, gb, :, k * 128:(k + 1) * 128], ("hT", gb, k)

        def F_pre_a(ti):
            sl = (ti + 1) % XS
            col = 1 + (ti + 1)
            xk = ("xs", sl)
            S.add("act", lambda e: e.activation(out=junk[:], in_=xs[:, sl, :], func=AF.Square, scale=1.0 / 32,
                                                accum_out=stat[:, col:col + 1]),
                  reads=[xk], writes=["junk", ("stat", col)])
            S.add("pool", lambda e: e.tensor_scalar(out=stat[:, col:col + 1], in0=stat[:, col:col + 1],
                                                    scalar1=float(EPS), scalar2=None, op0=ALU.add),
                  writes=[("stat", col)])
            S.add("pool", lambda e: e.tensor_tensor(out=stat[:, col:col + 1], in0=stat[:, col:col + 1],
                                                    in1=stat[:, 0:1], op=ALU.pow),
                  reads=["c_m05"], writes=[("stat", col)])

        def F_pre_b(ti):
            sl = (ti + 1) % XS
            col = 1 + (ti + 1)
            hb = (ti + 1) % 2
            S.add("dve", lambda e: e.scalar_tensor_tensor(out=hbf[:, hb, :], in0=xs[:, sl, :],
                                                          scalar=stat[:, col:col + 1], in1=g32[:],
                                                          op0=ALU.mult, op1=ALU.mult),
                  reads=[("xs", sl), ("stat", col), "g32"], writes=[("hbf", hb)])

        def F_tr(ti):
            hb = (ti + 1) % 2
            psb = nbank()
            pb = psbf(psb).rearrange("p (c t) -> p c t", c=8)

            def tr(e):
                for c in range(8):
                    ins = e.transpose(out=pb[:, c, :], in_=hbf[:, hb, c * 128:(c + 1) * 128], identity=identb[:])
                return ins
            S.add("pe", tr, reads=[("hbf", hb), "identb"], writes=[PSB(psb)])
            dst, dkey = hT_dst(ti)
            S.add("act", lambda e: e.activation(out=dst, in_=pb, func=AF.Copy), writes=[PSB(psb), dkey])

        def F_mm(ti):
            halo = ti < 0
            src, skey = hT_dst(ti)
            tix = ti + 1
            qb = tix % 2
            psq = None if halo else nbank()
            pskv = nbank()

            def mm(e):
                ins = None
                for c in range(8):
                    if not halo:
                        e.matmul(ps[psq][:, :], lhsT=src[:, c, :], rhs=winb[:, c, C_Q:C_Q + 512],
                                 start=(c == 0), stop=(c == 7))
                    ins = e.matmul(ps[pskv][:, 0:256], lhsT=src[:, c, :], rhs=winb[:, c, C_K:C_K + 256],
                                   start=(c == 0), stop=(c == 7))
                return ins
            S.add("pe", mm, reads=[skey] + W_KV + ([] if halo else W_Q), writes=[PSB(pskv)] + ([] if halo else [PSB(psq)]))
            if not halo:
                S.add("act", lambda e: e.activation(out=qk_sb[:, 0:512], in_=ps[psq][:, :], func=AF.Copy),
                      writes=[PSB(psq), "qk_q"])
            S.add("act", lambda e: e.activation(out=qk_sb[:, 512:640], in_=ps[pskv][:, 0:128], func=AF.Copy),
                  writes=[PSB(pskv), "qk_k"])
            S.add("act", lambda e: e.activation(out=vaug[:, tix, :, 0:64],
                                                in_=ps[pskv][:, 128:256].rearrange("p (k d) -> p k d", k=2),
                                                func=AF.Copy),
                  writes=[PSB(pskv), ("vaug", tix)])
            h0 = 8 if halo else 0
            nh = 10 - h0
            c0 = h0 * 64

            def v4(t):
                return t[:, c0:640].rearrange("p (h t d) -> p h t d", t=2, d=32)

            def bc(tab, n2):
                a = tab[:, tix, :].unsqueeze(1)
                if n2:
                    a = a.unsqueeze(1)
                    return a.to_broadcast([128, nh, 2, 32])
                return a.to_broadcast([128, nh, 32])
            qkr = ["qk_k"] if halo else ["qk_q", "qk_k"]
            S.add("dve", lambda e: e.tensor_tensor(out=v4(ropeA), in0=v4(qk_sb), in1=bc(cs, True), op=ALU.mult),
                  reads=qkr + ["cs"], writes=["ropeA"])
            S.add("dve", lambda e: e.tensor_tensor(out=v4(ropeB)[:, :, 0, :], in0=v4(qk_sb)[:, :, 1, :],
                                                    in1=bc(nsn, False), op=ALU.mult),
                  reads=qkr + ["nsn"], writes=["ropeB0"])
            S.add("dve", lambda e: e.tensor_tensor(out=v4(ropeB)[:, :, 1, :], in0=v4(qk_sb)[:, :, 0, :],
                                                    in1=bc(sn, False), op=ALU.mult),
                  reads=qkr + ["sn"], writes=["ropeB1"])
            S.add("dve", lambda e: e.tensor_tensor(out=qrot[:, qb, c0:640], in0=ropeA[:, c0:640],
                                                   in1=ropeB[:, c0:640], op=ALU.add),
                  reads=["ropeA", "ropeB0", "ropeB1"], writes=[("qrot", qb)])

        def F_u(ti):
            src, skey = hT_dst(ti)
            tix = ti + 1
            us = tix % 3
            psu = nbank()

            def mm(e):
                for c in range(8):
                    ins = e.matmul(ps[psu][:, :], lhsT=src[:, c, :], rhs=winb[:, c, C_U:C_U + 512],
                                   start=(c == 0), stop=(c == 7))
                return ins
            S.add("pe", mm, reads=[skey] + W_U, writes=[PSB(psu)])
            S.add("act", lambda e: e.activation(out=u_tm[:, us, :], in_=ps[psu][:, :], func=AF.Copy),
                  writes=[PSB(psu), ("utm", us)])

        def F_D(ti):
            gb, k = (ti // 4) % 2, ti % 4
            tix = ti + 1
            cur, prv = tix % 3, (tix - 1) % 3
            psd = nbank()

            def mm(e):
                for g in range(4):
                    bc_i = (8 + g) if ti == 0 else 2 * g
                    e.matmul(ps[psd][:, g * 128:(g + 1) * 128], lhsT=u_tm[:, cur, g * 128:(g + 1) * 128],
                             rhs=bandb[:, bc_i, :], start=True, stop=False)
                    ins = e.matmul(ps[psd][:, g * 128:g * 128 + 16], lhsT=u_tm[:, prv, g * 128:(g + 1) * 128],
                                   rhs=bandb[:, 2 * g + 1, 0:16], start=False, stop=True)
                return ins
            S.add("pe", mm, reads=[("utm", cur), ("utm", prv), "bandb"], writes=[PSB(psd)])
            S.add("act", lambda e: e.activation(out=dT[:, gb, :, k * 128:(k + 1) * 128],
                                                in_=ps[psd][:, :].rearrange("p (g t) -> p g t", g=4), func=AF.Copy),
                  writes=[PSB(psd), ("dT", gb, k)])

        def F_tr2(ti):
            halo = ti < 0
            tix = ti + 1
            qb = tix % 2
            pst = nbank()
            pb = psbf(pst)[:, 0:640].rearrange("p (c t) -> p c t", c=5)
            cl = [4] if halo else [0, 1, 2, 3, 4]

            def tr(e):
                for c in cl:
                    ins = e.transpose(out=pb[:, c, :], in_=qrot[:, qb, c * 128:(c + 1) * 128], identity=identb[:])
                return ins
            S.add("pe", tr, reads=[("qrot", qb), "identb"], writes=[PSB(pst)])
            if not halo:
                gb, k = (ti // 4) % 2, ti % 4
                S.add("act", lambda e: e.activation(out=qT[:, gb, :, k * 128:(k + 1) * 128], in_=pb[:, 0:4, :],
                                                    func=AF.Copy),
                      writes=[PSB(pst), ("qT", gb, k)])
            S.add("act", lambda e: e.activation(out=kT[:, tix * 128:(tix + 1) * 128], in_=pb[:, 4, :], func=AF.Copy),
                  writes=[PSB(pst), ("kT", tix)])

        def M_chunk(G, cc):
            gb = G % 2
            col = C_GA + cc * 128
            psb = nbank()
            tb = cc % 2

            def mm(e):
                for c in range(8):
                    ins = e.matmul(ps[psb][:, :], lhsT=winb[:, c, col:col + 128], rhs=hT[:, gb, c, :],
                                   start=(c == 0), stop=(c == 7))
                return ins
            S.add("pe", mm, reads=[("hT", gb, k) for k in range(4)] + WINB[cc // 4], writes=[PSB(psb)])
            dstt, dkey = (sga, ("sga", cc)) if cc < 4 else (sgp, ("sgp", cc - 4))
            ci = cc % 4
            S.add("act", lambda e: e.activation(out=tsb[:, tb, :], in_=ps[psb][:, :], func=AF.Tanh, scale=0.5),
                  writes=[PSB(psb), ("tsb", tb)])
            S.add("dve", lambda e: e.scalar_tensor_tensor(out=dstt[:, ci, :], in0=tsb[:, tb, :], scalar=1.0,
                                                          in1=ps[psb][:, :], op0=ALU.add, op1=ALU.mult),
                  reads=[("tsb", tb)], writes=[PSB(psb), dkey])

        def A_qk(ti):
            b4 = [nbank() for _ in range(4)]
            SC_BANKS = {(0, 0): b4[0], (1, 0): b4[1], (0, 1): b4[2], (1, 1): b4[3]}
            gb, k = (ti // 4) % 2, ti % 4
            tix = ti + 1
            pbuf = ti % 2
            for ch in (0, 1):
                mi = 2 if ch == 1 else (0 if ti == 0 else 1)
                for kv in (0, 1):
                    bank = SC_BANKS[(kv, ch)]
                    ktile = tix - 1 + ch
                    S.add("pe", lambda e, kv=kv, bank=bank, ktile=ktile: e.matmul(
                        ps[bank][:, :],
                        lhsT=kT[kv * 64:(kv + 1) * 64, ktile * 128:(ktile + 1) * 128],
                        rhs=qT[kv * 64:(kv + 1) * 64, gb, :, k * 128:(k + 1) * 128].rearrange(
                            "p g (h q) -> p h g q", h=2),
                        start=True, stop=False),
                        reads=[("kT", ktile), ("qT", gb, k)], writes=[PSB(bank)])
                for h in (0, 1):
                    for kv in (0, 1):
                        bank = SC_BANKS[(kv, ch)]
                        S.add("pe", lambda e, bank=bank, mi=mi, h=h, kv=kv: e.matmul(
                            ps[bank][:, h * 256:(h + 1) * 256],
                            lhsT=atab[kv * 64:(kv + 1) * 64, mi, h, :],
                            rhs=btab[kv * 64:(kv + 1) * 64, :].unsqueeze(1).to_broadcast([64, 4, 64]),
                            start=False, stop=(h == 1)),
                            reads=["atab", "btab"], writes=[PSB(bank)])
            for kv in (0, 1):
                for ch in (0, 1):
                    bank = SC_BANKS[(kv, ch)]
                    slot = kv * 2 + ch
                    S.add("act", lambda e, bank=bank, slot=slot: e.activation(
                        out=PT[:, pbuf, slot, :].rearrange("p (g h q) -> p h g q", g=4, h=2),
                        in_=ps[bank][:, :].rearrange("p (h g q) -> p h g q", h=2, g=4), func=AF.Exp, scale=0.125),
                        writes=[PSB(bank), ("PT", pbuf, slot)])

        def A_pv(ti, kv):
            tix = ti + 1
            pbuf = ti % 2
            bank = nbank()

            def mm(e):
                for j in range(4):
                    for ch in (0, 1):
                        ins = e.matmul(ps[bank][:, j * 65:(j + 1) * 65],
                                       lhsT=PT[:, pbuf, kv * 2 + ch, j * 128:(j + 1) * 128],
                                       rhs=vaug[:, tix - 1 + ch, kv, 0:65],
                                       start=(ch == 0), stop=(ch == 1))
                return ins
            S.add("pe", mm, reads=[("PT", pbuf, kv * 2), ("PT", pbuf, kv * 2 + 1), ("vaug", tix - 1), ("vaug", tix)],
                  writes=[PSB(bank)])
            pv = ps[bank][:, 0:260].rearrange("p (j d) -> p j d", d=65)
            dsl = den[:, pbuf, kv * 4:(kv + 1) * 4]
            S.add("dve", lambda e: e.scalar_tensor_tensor(
                out=dsl.unsqueeze(2), in0=pv[:, :, 64:65], scalar=2.0,
                in1=esink2[:, kv * 4:(kv + 1) * 4].unsqueeze(2), op0=ALU.mult, op1=ALU.add),
                reads=["esink2"], writes=[PSB(bank), ("den", pbuf, kv)])
            S.add("dve", lambda e: e.reciprocal(out=dsl, in_=dsl), writes=[("den", pbuf, kv)])
            S.add("dve", lambda e: e.tensor_tensor(
                out=a_tm[:, pbuf, kv * 256:(kv + 1) * 256].rearrange("p (j d) -> p j d", d=64),
                in0=pv[:, :, 0:64], in1=dsl.unsqueeze(2).to_broadcast([128, 4, 64]), op=ALU.mult),
                reads=[("den", pbuf, kv)], writes=[PSB(bank), ("a_tm", pbuf, kv)])

        def A_tr(ti):
            gb, k = (ti // 4) % 2, ti % 4
            pbuf = ti % 2
            pst = nbank()
            pb = psbf(pst)[:, 0:512].rearrange("p (c t) -> p c t", c=4)

            def tr(e):
                for c in range(4):
                    ins = e.transpose(out=pb[:, c, :], in_=a_tm[:, pbuf, c * 128:(c + 1) * 128], identity=identb[:])
                return ins
            S.add("pe", tr, reads=[("a_tm", pbuf, 0), ("a_tm", pbuf, 1), "identb"], writes=[PSB(pst)])
            S.add("dve", lambda e: e.tensor_tensor(out=mixT[:, gb, 0:4, k * 128:(k + 1) * 128], in0=pb,
                                                   in1=sga[:, :, k * 128:(k + 1) * 128], op=ALU.mult),
                  reads=[("sga", c) for c in range(4)], writes=[PSB(pst), ("mixA", gb, k)])

        def P_mm(G, g):
            gb = G % 2
            psb = nbank()
            S.add("pe", lambda e: e.matmul(ps[psb][:, :], lhsT=wpoolb[:, g, :], rhs=dT[:, gb, g, :],
                                           start=True, stop=True),
                  reads=[("dT", gb, k) for k in range(4)] + ["wpoolb"], writes=[PSB(psb)])
            S.add("dve", lambda e: e.scalar_tensor_tensor(
                out=mixT[:, gb, 4 + g, :], in0=ps[psb][:, :], scalar=psc05[:, g:g + 1], in1=sgp[:, g, :],
                op0=ALU.mult, op1=ALU.mult),
                reads=["psc05", ("sgp", g)], writes=[PSB(psb), ("mixP", gb, g)])

        def O_half(ti, n):
            gb, k = (ti // 4) % 2, ti % 4
            sl = ti % XR
            psy = nbank()

            def mm(e):
                for c in range(8):
                    ins = e.matmul(ps[psy][:, :], lhsT=mixT[:, gb, c, k * 128:(k + 1) * 128],
                                   rhs=woutb[:, c, n * 512:(n + 1) * 512], start=(c == 0), stop=(c == 7))
                return ins
            S.add("pe", mm, reads=[("mixA", gb, k)] + [("mixP", gb, g) for g in range(4)] + WOUT[n],
                  writes=[PSB(psy)])
            S.add("dve", lambda e: e.tensor_tensor(out=xr[:, sl, n * 512:(n + 1) * 512], in0=ps[psy][:, :],
                                                   in1=xr[:, sl, n * 512:(n + 1) * 512], op=ALU.add),
                  writes=[PSB(psy), ("xr", sl, n)])

        def O_tail_a(ti):
            sl = ti % XR
            col = 20 + ti
            yk = [("xr", sl, 0), ("xr", sl, 1)]
            S.add("act", lambda e: e.activation(out=junk[:], in_=xr[:, sl, :], func=AF.Square, scale=1.0 / 32,
                                                accum_out=stat[:, col:col + 1]),
                  reads=yk, writes=["junk", ("stat", col)])
            S.add("pool", lambda e: e.tensor_scalar(out=stat[:, col:col + 1], in0=stat[:, col:col + 1],
                                                    scalar1=float(EPS), scalar2=None, op0=ALU.add),
                  writes=[("stat", col)])
            S.add("pool", lambda e: e.tensor_tensor(out=stat[:, col:col + 1], in0=stat[:, col:col + 1],
                                                    in1=stat[:, 0:1], op=ALU.pow),
                  reads=["c_m05"], writes=[("stat", col)])

        def O_tail_b(ti):
            sl = ti % XR
            col = 20 + ti
            yk = [("xr", sl, 0), ("xr", sl, 1)]
            S.add("dve", lambda e: e.scalar_tensor_tensor(out=xr[:, sl, :], in0=xr[:, sl, :],
                                                          scalar=stat[:, col:col + 1], in1=gfr[:],
                                                          op0=ALU.mult, op1=ALU.mult),
                  reads=[("stat", col), "gfr"], writes=yk)
            dma("sp", ("y", sl), out_d[ti * 128:(ti + 1) * 128, :], xr[:, sl, :],
                reads=yk, writes=[("out", ti)])

        def late_setup():
            setup_late()
            S.add("act", lambda e: e.activation(out=esink2[:], in_=esink2[:], func=AF.Exp), writes=["esink2"])
            S.add("dve", lambda e: e.tensor_scalar(out=esink2[:], in0=esink2[:], scalar1=2.0, scalar2=None,
                                                   op0=ALU.mult), writes=["esink2"])

        pending = []

        def tick():
            for p in pending:
                p[0] -= 1
            while pending and pending[0][0] <= 0:
                pending.pop(0)[1]()

        def later(n, fn):
            pending.append([n, fn])

        def pre(ti, delay=2):
            if ti >= NT:
                return
            F_pre_a(ti)

            def fb():
                F_pre_b(ti)
                if XS - 1 <= ti + XS - 1 < NT:
                    load_xs(ti + XS - 1)
            later(delay, fb)

        def otail(ti, delay=2):
            O_tail_a(ti)

            def fb():
                O_tail_b(ti)
                if ti + XR < NT:
                    load_xr(ti + XR)
            later(delay, fb)

        def flush():
            while pending:
                pending.pop(0)[1]()

        def preb(ti):
            F_pre_b(ti)
            if XS - 1 <= ti + XS - 1 < NT:
                load_xs(ti + XS - 1)
        wdma("wq", C_Q, C_Q + 512)
        F_pre_a(-1); F_pre_a(0)
        setup_rope()
        F_pre_a(1)
        preb(-1); preb(0)
        wdma("wga", C_GA, C_GA + 512)
        F_pre_a(2)
        F_tr(-1); F_tr(0)
        preb(1)
        F_mm(-1)
        F_tr(1)
        F_pre_a(3)
        preb(2)
        F_mm(0); F_tr2(-1)
        F_tr(2)
        F_pre_a(4)
        dma_tables()
        wdma("wu", C_U, C_U + 512)
        wdma("wgp", C_GP, C_GP + 512)
        dma_wpool()
        preb(3)
        F_mm(1); F_tr2(0)
        F_tr(3)
        preb(4)
        F_mm(2); F_tr2(1)
        late_setup()
        F_mm(3); F_tr2(2)
        F_tr(4)
        for ti in range(XR):
            load_xr(ti)

        atr_done = set()
        pmm_done = {}

        for G in range(NG):
            has_f = G + 1 < NG
            has_o = G >= 1
            last = G == NG - 1
            at = [4 * G + k for k in range(4)]
            ft = [4 * (G + 1) + k for k in range(4)]
            ot = [4 * (G - 1) + k for k in range(4)]
            heavy = [("ga", 0), ("ga", 1), ("ga", 2), ("ga", 3)]
            fd3 = {"done": G != 0}
            gp_wait = []
            if G == 0:
                heavy += [("fu", -1), ("fu", 0), ("gp", 0), ("fu", 1), ("fd", 0), ("gp", 1), ("fu", 2), ("fd", 1),
                          ("gp", 2), ("fu", 3), ("fd", 2), ("gp", 3), ("fd", 3), ("fu", 4), ("fu", 5), ("fd", 4),
                          ("fu", 6), ("fd", 5), ("fu", 7), ("fd", 6)]
            elif last:
                heavy += [("gp", 0), ("gp", 1), ("gp", 2), ("gp", 3)]
                for k in range(4):
                    heavy += [("oh", ot[k], 0), ("oh", ot[k], 1)]
                for k in range(4):
                    heavy += [("oh", at[k], 0), ("oh", at[k], 1)]
            elif has_o:
                for k in range(4):
                    heavy += [("oh", ot[k], 0), ("oh", ot[k], 1), ("gp", k)]
            else:
                heavy += [("gp", 0), ("gp", 1), ("gp", 2), ("gp", 3)]
            pm_ready = []
            pmm_done[G] = 0

            def ready(h):
                if h[0] != "oh":
                    return True
                ti = h[1]
                if ti // 4 == G:
                    return ti in atr_done and pmm_done[G] == 4
                return True

            def take(do_tick=False):
                if do_tick:
                    tick()
                if not heavy or not ready(heavy[0]):
                    return
                h = heavy.pop(0)
                if h[0] == "oh":
                    O_half(h[1], h[2])
                    if h[2] == 1:
                        otail(h[1], delay=1)
                elif h[0] == "ga":
                    M_chunk(G, h[1])
                elif h[0] == "fu":
                    F_u(h[1])
                elif h[0] == "fd":
                    F_D(h[1])
                    if h[1] == 3:
                        fd3["done"] = True
                        pm_ready.extend(gp_wait)
                        del gp_wait[:]
                else:
                    M_chunk(G, 4 + h[1])
                    (pm_ready if fd3["done"] else gp_wait).append(h[1])

            def pmm():
                while pm_ready:
                    P_mm(G, pm_ready.pop(0))
                    pmm_done[G] += 1

            for i in range(4):
                nxt = ft[i] + 1
                if has_f and nxt < NT:
                    pre(nxt, delay=1)
                A_qk(at[i])
                take()
                if has_f and G > 0:
                    F_u(ft[i])
                A_pv(at[i], 0)
                take()
                A_pv(at[i], 1)
                if i == 0:
                    take()
                    take()
                else:
                    take()
                A_tr(at[i])
                atr_done.add(at[i])
                tick()
                if has_f:
                    F_mm(ft[i])
                take()
                if i == 0:
                    F_tr2(at[3])
                    if G > 0:
                        F_D(at[3])
                elif has_f:
                    F_tr2(ft[i - 1])
                    if G > 0:
                        F_D(ft[i - 1])
                take(True)
                if has_f and nxt < NT:
                    F_tr(nxt)
                take(True)
                pmm()
                if G == 0:
                    dma_wout(2 * i, 2 * i + 2)
                if G == 0 and i == 0:
                    dma("sp", "gf", gfr[:], gf_d, writes=["gfr"])
            guard = 0
            while heavy and guard < 100:
                take(True)
                guard += 1
            assert not heavy
            pmm()
            flush()
        S.add("sp", lambda e: e.nop(), reads=[("out", ti) for ti in range(NT)])

        S.emit(nc, es)
    return nc


_NC_CACHE = {}


def _host_tables(seg):
    half = 32
    inv_freq = np.array([
        0x3f800000, 0x3f3ff911, 0x3f0ff59a, 0x3ed7e89b, 0x3ea1e89b, 0x3e72d423, 0x3e361887, 0x3e088d77,
        0x3dcccccd, 0x3d99940d, 0x3d6655c2, 0x3d2cba15, 0x3d0186e3, 0x3cc2434f, 0x3c91ad39, 0x3c5a7bf2,
        0x3c23d70a, 0x3bf5b9b0, 0x3bb8449c, 0x3b8a2e77, 0x3b4f3e38, 0x3b1b690d, 0x3ae91528, 0x3aaec98e,
        0x3a83126f, 0x3a44948c, 0x3a136a16, 0x39dd1725, 0x39a5cb60, 0x3978a815, 0x393a7753, 0x390bd472,
    ], dtype=np.uint32).view(np.float32)
    pos = (seg * T - 128 + np.arange((NT + 1) * 128)).astype(np.float32)
    ang = (pos[:, None] * inv_freq[None, :]).astype(np.float32)
    cos_t = np.cos(ang).astype(np.float32)
    sin_t = np.sin(ang).astype(np.float32)
    cos_t = np.ascontiguousarray(cos_t.reshape(NT + 1, 128, 32).transpose(1, 0, 2).reshape(128, (NT + 1) * 32))
    sin_t = np.ascontiguousarray(sin_t.reshape(NT + 1, 128, 32).transpose(1, 0, 2).reshape(128, (NT + 1) * 32))
    j = np.arange(128)[:, None]
    i = np.arange(128)[None, :]
    NEG = np.float32(-30000.0)
    m_cur = np.where(j <= i, np.float32(0.0), NEG).astype(np.float32)
    m_prev = np.where(j > i, np.float32(0.0), NEG).astype(np.float32)
    m_prev0 = m_prev if seg != 0 else np.full((128, 128), NEG, np.float32)
    masks = np.stack([m_prev0, m_prev, m_cur], axis=1).astype(np.float32)
    atab = np.empty((128, 3, 2, 128), np.float32)
    for p in range(128):
        for h in range(2):
            atab[p, :, h, :] = masks[:, :, (p % 64) + 64 * h].T
    btab = (np.arange(128)[:, None] % 64 == np.arange(64)[None, :]).astype(np.float32)
    tp = np.arange(128)[:, None]
    tq = np.arange(128)[None, :]
    band = np.zeros((128, 12, 128), np.float32)
    for g, w in enumerate(WINDOWS):
        inwin = ((tq - tp) >= 0) & ((tq - tp) < w)
        cur = np.where(inwin, np.float32(1.0 / w), np.float32(0.0)) - (tp == tq).astype(np.float32)
        prev = np.where((tq + 128 - tp) < w, np.float32(1.0 / w), np.float32(0.0))
        band[:, 2 * g, :] = cur
        band[:, 2 * g + 1, :] = prev
        if seg == 0:
            cnt = np.minimum(tq + 1, w).astype(np.float32)
            cur0 = np.where(inwin, np.float32(1.0) / cnt, np.float32(0.0)) - (tp == tq).astype(np.float32)
        else:
            cur0 = cur
        band[:, 8 + g, :] = cur0
    return cos_t, sin_t, (atab, btab), band


def kernel(x, norm_in, w_in, attn_sinks, w_pool, pool_scale, w_out, norm_final):
    x = np.asarray(x, np.float32)
    w_in0 = np.asarray(w_in, np.float32)[0]
    qperm = []
    for j in range(4):
        for kv in range(2):
            h = kv * 4 + j
            qperm.extend(range(h * 64, (h + 1) * 64))
    cols = np.concatenate([np.array(qperm), np.arange(512, 768), np.arange(1280, 1792), np.arange(768, 1280),
                           np.arange(1792, DIN)])
    w_in_p = np.ascontiguousarray(w_in0[:, cols])
    w_out0 = np.ascontiguousarray(np.asarray(w_out, np.float32)[0])
    w_pool0 = np.ascontiguousarray(np.asarray(w_pool, np.float32)[0])
    g_row = np.ascontiguousarray(np.asarray(norm_in, np.float32)[0][None, :])
    gf_row = np.ascontiguousarray(np.asarray(norm_final, np.float32)[None, :])
    sink_rep = np.ascontiguousarray(np.broadcast_to(np.asarray(attn_sinks, np.float32)[0][None, :], (128, 8)))
    pscale = np.ascontiguousarray(np.asarray(pool_scale, np.float32)[0].reshape(4, 128).T)
    ident = np.eye(128, dtype=np.float32)

    in_maps = []
    for c in range(N_CORES):
        b, seg = c // 4, c % 4
        xc = np.ascontiguousarray(x[b, seg * T:(seg + 1) * T, :])
        if seg == 0:
            xh = np.zeros((128, D), np.float32)
        else:
            xh = np.ascontiguousarray(x[b, seg * T - 128:seg * T, :])
        cos_t, sin_t, masks, band = _host_tables(seg)
        in_maps.append({
            "x": xc, "xh": xh, "w_in": w_in_p, "w_out": w_out0, "w_pool": w_pool0,
            "g_row": g_row, "gf_row": gf_row, "sink_rep": sink_rep, "pscale": pscale,
            "cos_t": cos_t, "sin_t": sin_t, "atab": masks[0], "btab": masks[1], "band": band, "ident": ident,
        })
    if "nc" not in _NC_CACHE:
        _NC_CACHE["nc"] = build_nc()
    nc = _NC_CACHE["nc"]
    res = run_bass_kernel_spmd(nc, in_maps, core_ids=list(range(N_CORES)))
    out = np.empty((BATCH, SEQ, D), np.float32)
    for c in range(N_CORES):
        b, seg = c // 4, c % 4
        out[b, seg * T:(seg + 1) * T, :] = res.results[c]["out"]
    return out
```

```python
import math
from contextlib import ExitStack

import numpy as np
import concourse.bass as bass
import concourse.mybir as mybir
from concourse.bass_utils import run_bass_kernel_spmd

F32 = mybir.dt.float32
BF16 = mybir.dt.bfloat16
ALU = mybir.AluOpType
AF = mybir.ActivationFunctionType

N_CORES = 8
D = 1024
SEQ = 8192
BATCH = 2
T = 2048
NT = T // 128
NG = 4
DIN = 2304
C_Q, C_K, C_V, C_U, C_GA, C_GP = 0, 512, 640, 768, 1280, 1792
EPS = 1e-5
XSLOTS = 8
VW = 80
WINDOWS = (2, 4, 8, 16)


class Op:
    __slots__ = ("eng", "fn", "deps", "dma", "sig", "idx")


class Sched:
    def __init__(self):
        self.ops = []
        self.last_w = {}
        self.readers = {}

    def add(self, eng, fn, reads=(), writes=(), dma=None):
        op = Op()
        op.eng, op.fn, op.dma, op.sig, op.idx = eng, fn, dma, None, len(self.ops)
        deps = set()
        for r in reads:
            if r not in self.last_w:
                raise RuntimeError("read of never-written resource %r" % (r,))
            deps.add(self.last_w[r])
        for w in writes:
            if w in self.last_w:
                deps.add(self.last_w[w])
            for rd in self.readers.get(w, ()):
                deps.add(rd)
        for r in reads:
            self.readers.setdefault(r, []).append(op.idx)
        for w in writes:
            self.last_w[w] = op.idx
            self.readers[w] = []
        op.deps = sorted(deps)
        self.ops.append(op)
        return op.idx

    def emit(self, nc, es):
        ops = self.ops
        needed = [False] * len(ops)
        for op in ops:
            for d in op.deps:
                needed[d] = True
        cnt = {}
        sem_keys = []
        for op in ops:
            if op.dma is not None:
                k = ("dma", op.dma)
            elif needed[op.idx]:
                k = ("eng", op.eng)
            else:
                continue
            if k not in cnt:
                cnt[k] = 0
                sem_keys.append(k)
            cnt[k] += 16 if op.dma is not None else 1
            op.sig = (k, cnt[k])
        sems = {}
        for i, k in enumerate(sem_keys):
            sems[k] = es.enter_context(nc.semaphore("s%d" % i))
        block = es.enter_context(nc.Block())
        by_eng = {}
        for op in ops:
            by_eng.setdefault(op.eng, []).append(op)

        def run(eng_name, eng):
            known = {}
            for op in by_eng.get(eng_name, []):
                want = {}
                for d in op.deps:
                    dop = ops[d]
                    if dop.dma is None and dop.eng == eng_name and eng_name == "pe":
                        continue
                    k, v = dop.sig
                    if v > want.get(k, 0):
                        want[k] = v
                for k, v in want.items():
                    if known.get(k, 0) >= v:
                        continue
                    eng.wait_ge(sems[k], v)
                    known[k] = v
                ins = op.fn(eng)
                if op.sig is not None:
                    k, v = op.sig
                    ins.then_inc(sems[k], 16 if op.dma is not None else 1)

        @block.sync
        def _(e):
            run("sp", e)

        @block.scalar
        def _(e):
            run("act", e)

        @block.vector
        def _(e):
            run("dve", e)

        @block.gpsimd
        def _(e):
            run("pool", e)

        @block.tensor
        def _(e):
            run("pe", e)


def build_nc(debug=False):
    nc = bass.Bass("TRN2", target_bir_lowering=False)

    def din(name, shape):
        return nc.dram_tensor(name, list(shape), F32, kind="ExternalInput").ap()

    x_d = din("x", [T, D])
    xh_d = din("xh", [128, D])
    win_d = din("w_in", [D, DIN])
    wout_d = din("w_out", [D, D])
    wpool_d = din("w_pool", [4, 128, 128])
    g_d = din("g_row", [1, D]).partition_broadcast(128)
    gf_d = din("gf_row", [1, D]).partition_broadcast(128)
    sink_d = din("sink_rep", [128, 8])
    psc_d = din("pscale", [128, 4])
    cos_d = din("cos_t", [128, (NT + 1) * 32])
    sin_d = din("sin_t", [128, (NT + 1) * 32])
    atab_d = din("atab", [128, 3, 2, 128])
    btab_d = din("btab", [128, 64])
    band_d = din("band", [128, 12, 128])
    ident_d = din("ident", [128, 128])
    out_d = nc.dram_tensor("out", [T, D], F32, kind="ExternalOutput").ap()

    S = Sched()
    with ExitStack() as es:
        E = es.enter_context

        def sb(name, shape, dt):
            return E(nc.sbuf_tensor("sb_" + name, list(shape), dt))

        XS, XR = 4, 4
        xs = sb("xs", [128, XS, D], F32)
        xr = sb("xr", [128, XR, D], F32)
        junk = sb("junk", [128, D], BF16)
        hbf = sb("hbf", [128, 2, D], BF16)
        hT = sb("hT", [128, 2, 8, 512], BF16)
        hTh = sb("hTh", [128, 8, 128], BF16)
        winb = sb("winb", [128, 8, DIN], BF16)
        woutb = sb("woutb", [128, 8, D], BF16)
        wpoolb = sb("wpoolb", [128, 4, 128], BF16)
        qk_sb = sb("qk_sb", [128, 640], F32)
        ropeA = sb("ropeA", [128, 640], F32)
        ropeB = sb("ropeB", [128, 640], F32)
        qrot = sb("qrot", [128, 2, 640], BF16)
        qT = sb("qT", [128, 2, 4, 512], BF16)
        kT = sb("kT", [128, (NT + 1) * 128], BF16)
        vaug = sb("vaug", [128, NT + 1, 2, VW], BF16)
        tsb = sb("tsb", [128, 2, 512], F32)
        sga = sb("sga", [128, 4, 512], BF16)
        sgp = sb("sgp", [128, 4, 512], BF16)
        u_tm = sb("u_tm", [128, 3, 512], BF16)
        bandb = sb("bandb", [128, 12, 128], BF16)
        dT = sb("dT", [128, 2, 4, 512], BF16)
        PT = sb("PT", [128, 2, 4, 512], BF16)
        a_tm = sb("a_tm", [128, 2, 512], BF16)
        den = sb("den", [128, 2, 8], F32)
        mixT = sb("mixT", [128, 2, 8, 512], BF16)
        cs = sb("cs", [128, NT + 1, 32], F32)
        sn = sb("sn", [128, NT + 1, 32], F32)
        nsn = sb("nsn", [128, NT + 1, 32], F32)
        atab = sb("atab", [128, 3, 2, 128], BF16)
        btab = sb("btab", [128, 64], BF16)
        g32 = sb("g32", [128, D], F32)
        gfr = sb("gfr", [128, D], F32)
        identb = sb("identb", [128, 128], BF16)
        esink2 = sb("esink2", [128, 8], F32)
        psc05 = sb("psc05", [128, 4], F32)
        stat = sb("stat", [128, 64], F32)

        ps = [E(nc.psum_tensor("ps%d" % i, [128, 512], F32)) for i in range(8)]
        rot = {"i": 0}

        def nbank():
            b = rot["i"] % 8
            rot["i"] += 1
            return b

        def PSB(i):
            return ("psum", i)

        def psbf(i):
            return ps[i][:].bitcast(BF16)

        def dma(eng, key, out, in_, reads=(), writes=()):
            S.add(eng, lambda e: e.dma_start(out=out, in_=in_), reads=reads, writes=writes, dma=key)

        def load_xs(ti):
            sl = (ti + 1) % XS
            src = xh_d if ti < 0 else x_d[ti * 128:(ti + 1) * 128, :]
            dma("sp", ("xs", sl), xs[:, sl, :], src, writes=[("xs", sl)])

        def load_xr(ti):
            sl = ti % XR
            dma("sp", ("xr", sl), xr[:, sl, :], x_d[ti * 128:(ti + 1) * 128, :],
                writes=[("xr", sl, 0), ("xr", sl, 1)])

        load_xs(-1)
        dma("sp", "g", g32[:], g_d, writes=["g32"])
        load_xs(0)
        dma("sp", "cos", cs[:], cos_d.rearrange("p (t d) -> p t d", d=32), writes=["cs"])
        dma("sp", "sin", sn[:], sin_d.rearrange("p (t d) -> p t d", d=32), writes=["sn"])
        load_xs(1)
        load_xs(2)
        win_v = win_d.rearrange("(c p) n -> p c n", p=128)
        wout_v = wout_d.rearrange("(c p) n -> p c n", p=128)

        def wkeys(tag):
            return [(tag, c) for c in range(8)]

        def wdma(tag, lo, hi):
            dma("pool", (tag, 0), winb[:, :, lo:hi], win_v[:, :, lo:hi], writes=[(tag, c) for c in range(8)])
        W_KV, W_Q, W_U, W_GA, W_GP = (wkeys(t_) for t_ in ("wkv", "wq", "wu", "wga", "wgp"))
        WOUT = [[("wout", c, n) for c in range(8)] for n in range(2)]
        WINB = [W_GA, W_GP]

        def dma_wpool():
            dma("pool", "wpool", wpoolb[:], wpool_d.rearrange("g c d -> c g d"), writes=["wpoolb"])

        def dma_wout(c0, c1):
            for c in range(c0, c1):
                dma("pool", ("wout", c), woutb[:, c, :], wout_v[:, c, :], writes=[("wout", c, 0), ("wout", c, 1)])

        def dma_tables():
            dma("pool", "atab", atab[:], atab_d, writes=["atab"])
            dma("pool", "btab", btab[:], btab_d, writes=["btab"])
            dma("pool", "band0", bandb[:, 0:8, :], band_d[:, 0:8, :], writes=["bandb"])
            dma("pool", "band1", bandb[:, 8:12, :], band_d[:, 8:12, :], writes=["bandb"])

        dma("pool", "ident", identb[:], ident_d, writes=["identb"])
        wdma("wkv", C_K, C_K + 256)

        dma("sp", "sink", esink2[:], sink_d, writes=["esink2"])
        dma("sp", "psc", psc05[:], psc_d, writes=["psc05"])

        S.add("dve", lambda e: e.memset(stat[:, 0:1], -0.5), writes=["c_m05"])
        S.add("act", lambda e: e.activation(out=stat[:, 62:63], in_=stat[:, 0:1], func=AF.Exp),
              reads=["c_m05"], writes=[("stat", 62)])

        def setup_rope():
            S.add("dve", lambda e: e.tensor_scalar(out=nsn[:], in0=sn[:], scalar1=-1.0, scalar2=None, op0=ALU.mult),
                  reads=["sn"], writes=["nsn"])
            S.add("pool", lambda e: e.memset(vaug[:], 1.0), writes=[("vaug", t) for t in range(NT + 1)])

        def setup_late():
            S.add("dve", lambda e: e.tensor_scalar(out=psc05[:], in0=psc05[:], scalar1=0.5, scalar2=None,
                                                   op0=ALU.mult), writes=["psc05"])

        def hT_dst(ti):
            if ti < 0:
                return hTh[:, :, :], ("hTh",)
            gb, k = (ti // 4) % 2, ti % 4
            return hT[:, gb, :, k * 128:(k + 1) * 128], ("hT", gb, k)

        def F_pre_a(ti):
            sl = (ti + 1) % XS
            col = 1 + (ti + 1)
            xk = ("xs", sl)
            S.add("act", lambda e: e.activation(out=junk[:], in_=xs[:, sl, :], func=AF.Square, scale=1.0 / 32,
                                                accum_out=stat[:, col:col + 1]),
                  reads=[xk], writes=["junk", ("stat", col)])
            S.add("pool", lambda e: e.tensor_scalar(out=stat[:, col:col + 1], in0=stat[:, col:col + 1],
                                                    scalar1=float(EPS), scalar2=None, op0=ALU.add),
                  writes=[("stat", col)])
            S.add("pool", lambda e: e.tensor_tensor(out=stat[:, col:col + 1], in0=stat[:, col:col + 1],
                                                    in1=stat[:, 0:1], op=ALU.pow),
                  reads=["c_m05"], writes=[("stat", col)])

        def F_pre_b(ti):
            sl = (ti + 1) % XS
            col = 1 + (ti + 1)
            hb = (ti + 1) % 2
            S.add("dve", lambda e: e.scalar_tensor_tensor(out=hbf[:, hb, :], in0=xs[:, sl, :],
                                                          scalar=stat[:, col:col + 1], in1=g32[:],
                                                          op0=ALU.mult, op1=ALU.mult),
                  reads=[("xs", sl), ("stat", col), "g32"], writes=[("hbf", hb)])

        def F_tr(ti):
            hb = (ti + 1) % 2
            psb = nbank()
            pb = psbf(psb).rearrange("p (c t) -> p c t", c=8)

            def tr(e):
                for c in range(8):
                    ins = e.transpose(out=pb[:, c, :], in_=hbf[:, hb, c * 128:(c + 1) * 128], identity=identb[:])
                return ins
            S.add("pe", tr, reads=[("hbf", hb), "identb"], writes=[PSB(psb)])
            dst, dkey = hT_dst(ti)
            S.add("act", lambda e: e.activation(out=dst, in_=pb, func=AF.Copy), writes=[PSB(psb), dkey])

        def F_mm(ti):
            halo = ti < 0
            src, skey = hT_dst(ti)
            tix = ti + 1
            qb = tix % 2
            psq = None if halo else nbank()
            pskv = nbank()

            def mm(e):
                ins = None
                for c in range(8):
                    if not halo:
                        e.matmul(ps[psq][:, :], lhsT=src[:, c, :], rhs=winb[:, c, C_Q:C_Q + 512],
                                 start=(c == 0), stop=(c == 7))
                    ins = e.matmul(ps[pskv][:, 0:256], lhsT=src[:, c, :], rhs=winb[:, c, C_K:C_K + 256],
                                   start=(c == 0), stop=(c == 7))
                return ins
            S.add("pe", mm, reads=[skey] + W_KV + ([] if halo else W_Q), writes=[PSB(pskv)] + ([] if halo else [PSB(psq)]))
            if not halo:
                S.add("act", lambda e: e.activation(out=qk_sb[:, 0:512], in_=ps[psq][:, :], func=AF.Copy),
                      writes=[PSB(psq), "qk_q"])
            S.add("act", lambda e: e.activation(out=qk_sb[:, 512:640], in_=ps[pskv][:, 0:128], func=AF.Copy),
                  writes=[PSB(pskv), "qk_k"])
            S.add("act", lambda e: e.activation(out=vaug[:, tix, :, 0:64],
                                                in_=ps[pskv][:, 128:256].rearrange("p (k d) -> p k d", k=2),
                                                func=AF.Copy),
                  writes=[PSB(pskv), ("vaug", tix)])
            h0 = 8 if halo else 0
            nh = 10 - h0
            c0 = h0 * 64

            def v4(t):
                return t[:, c0:640].rearrange("p (h t d) -> p h t d", t=2, d=32)

            def bc(tab, n2):
                a = tab[:, tix, :].unsqueeze(1)
                if n2:
                    a = a.unsqueeze(1)
                    return a.to_broadcast([128, nh, 2, 32])
                return a.to_broadcast([128, nh, 32])
            qkr = ["qk_k"] if halo else ["qk_q", "qk_k"]
            S.add("dve", lambda e: e.tensor_tensor(out=v4(ropeA), in0=v4(qk_sb), in1=bc(cs, True), op=ALU.mult),
                  reads=qkr + ["cs"], writes=["ropeA"])
            S.add("dve", lambda e: e.tensor_tensor(out=v4(ropeB)[:, :, 0, :], in0=v4(qk_sb)[:, :, 1, :],
                                                    in1=bc(nsn, False), op=ALU.mult),
                  reads=qkr + ["nsn"], writes=["ropeB0"])
            S.add("dve", lambda e: e.tensor_tensor(out=v4(ropeB)[:, :, 1, :], in0=v4(qk_sb)[:, :, 0, :],
                                                    in1=bc(sn, False), op=ALU.mult),
                  reads=qkr + ["sn"], writes=["ropeB1"])
            S.add("dve", lambda e: e.tensor_tensor(out=qrot[:, qb, c0:640], in0=ropeA[:, c0:640],
                                                   in1=ropeB[:, c0:640], op=ALU.add),
                  reads=["ropeA", "ropeB0", "ropeB1"], writes=[("qrot", qb)])

        def F_u(ti):
            src, skey = hT_dst(ti)
            tix = ti + 1
            us = tix % 3
            psu = nbank()

            def mm(e):
                for c in range(8):
                    ins = e.matmul(ps[psu][:, :], lhsT=src[:, c, :], rhs=winb[:, c, C_U:C_U + 512],
                                   start=(c == 0), stop=(c == 7))
                return ins
            S.add("pe", mm, reads=[skey] + W_U, writes=[PSB(psu)])
            S.add("act", lambda e: e.activation(out=u_tm[:, us, :], in_=ps[psu][:, :], func=AF.Copy),
                  writes=[PSB(psu), ("utm", us)])

        def F_D(ti):
            gb, k = (ti // 4) % 2, ti % 4
            tix = ti + 1
            cur, prv = tix % 3, (tix - 1) % 3
            psd = nbank()

            def mm(e):
                for g in range(4):
                    bc_i = (8 + g) if ti == 0 else 2 * g
                    e.matmul(ps[psd][:, g * 128:(g + 1) * 128], lhsT=u_tm[:, cur, g * 128:(g + 1) * 128],
                             rhs=bandb[:, bc_i, :], start=True, stop=False)
                    ins = e.matmul(ps[psd][:, g * 128:g * 128 + 16], lhsT=u_tm[:, prv, g * 128:(g + 1) * 128],
                                   rhs=bandb[:, 2 * g + 1, 0:16], start=False, stop=True)
                return ins
            S.add("pe", mm, reads=[("utm", cur), ("utm", prv), "bandb"], writes=[PSB(psd)])
            S.add("act", lambda e: e.activation(out=dT[:, gb, :, k * 128:(k + 1) * 128],
                                                in_=ps[psd][:, :].rearrange("p (g t) -> p g t", g=4), func=AF.Copy),
                  writes=[PSB(psd), ("dT", gb, k)])

        def F_tr2(ti):
            halo = ti < 0
            tix = ti + 1
            qb = tix % 2
            pst = nbank()
            pb = psbf(pst)[:, 0:640].rearrange("p (c t) -> p c t", c=5)
            cl = [4] if halo else [0, 1, 2, 3, 4]

            def tr(e):
                for c in cl:
                    ins = e.transpose(out=pb[:, c, :], in_=qrot[:, qb, c * 128:(c + 1) * 128], identity=identb[:])
                return ins
            S.add("pe", tr, reads=[("qrot", qb), "identb"], writes=[PSB(pst)])
            if not halo:
                gb, k = (ti // 4) % 2, ti % 4
                S.add("act", lambda e: e.activation(out=qT[:, gb, :, k * 128:(k + 1) * 128], in_=pb[:, 0:4, :],
                                                    func=AF.Copy),
                      writes=[PSB(pst), ("qT", gb, k)])
            S.add("act", lambda e: e.activation(out=kT[:, tix * 128:(tix + 1) * 128], in_=pb[:, 4, :], func=AF.Copy),
                  writes=[PSB(pst), ("kT", tix)])

        def M_chunk(G, cc):
            gb = G % 2
            col = C_GA + cc * 128
            psb = nbank()
            tb = cc % 2

            def mm(e):
                for c in range(8):
                    ins = e.matmul(ps[psb][:, :], lhsT=winb[:, c, col:col + 128], rhs=hT[:, gb, c, :],
                                   start=(c == 0), stop=(c == 7))
                return ins
            S.add("pe", mm, reads=[("hT", gb, k) for k in range(4)] + WINB[cc // 4], writes=[PSB(psb)])
            dstt, dkey = (sga, ("sga", cc)) if cc < 4 else (sgp, ("sgp", cc - 4))
            ci = cc % 4
            S.add("act", lambda e: e.activation(out=tsb[:, tb, :], in_=ps[psb][:, :], func=AF.Tanh, scale=0.5),
                  writes=[PSB(psb), ("tsb", tb)])
            S.add("dve", lambda e: e.scalar_tensor_tensor(out=dstt[:, ci, :], in0=tsb[:, tb, :], scalar=1.0,
                                                          in1=ps[psb][:, :], op0=ALU.add, op1=ALU.mult),
                  reads=[("tsb", tb)], writes=[PSB(psb), dkey])

        def A_qk(ti):
            b4 = [nbank() for _ in range(4)]
            SC_BANKS = {(0, 0): b4[0], (1, 0): b4[1], (0, 1): b4[2], (1, 1): b4[3]}
            gb, k = (ti // 4) % 2, ti % 4
            tix = ti + 1
            pbuf = ti % 2
            for ch in (0, 1):
                mi = 2 if ch == 1 else (0 if ti == 0 else 1)
                for kv in (0, 1):
                    bank = SC_BANKS[(kv, ch)]
                    ktile = tix - 1 + ch
                    S.add("pe", lambda e, kv=kv, bank=bank, ktile=ktile: e.matmul(
                        ps[bank][:, :].rearrange("p (j q) -> p j q", j=4),
                        lhsT=kT[kv * 64:(kv + 1) * 64, ktile * 128:(ktile + 1) * 128],
                        rhs=qT[kv * 64:(kv + 1) * 64, gb, :, k * 128:(k + 1) * 128],
                        start=True, stop=False),
                        reads=[("kT", ktile), ("qT", gb, k)], writes=[PSB(bank)])
                def mk(e, ch=ch, mi=mi):
                    ins = None
                    for h in (0, 1):
                        for g_ in range(4):
                            for kv in (0, 1):
                                bank = SC_BANKS[(kv, ch)]
                                c0 = g_ * 128 + h * 64
                                ins = e.matmul(ps[bank][:, c0:c0 + 64],
                                               lhsT=atab[kv * 64:(kv + 1) * 64, mi, h, :],
                                               rhs=btab[kv * 64:(kv + 1) * 64, :],
                                               start=False, stop=(h == 1 and g_ == 3))
                    return ins
                S.add("pe", mk, reads=["atab", "btab"], writes=[PSB(SC_BANKS[(0, ch)]), PSB(SC_BANKS[(1, ch)])])
            for kv in (0, 1):
                for ch in (0, 1):
                    bank = SC_BANKS[(kv, ch)]
                    slot = kv * 2 + ch
                    S.add("act", lambda e, bank=bank, slot=slot: e.activation(
                        out=PT[:, pbuf, slot, :], in_=ps[bank][:, :], func=AF.Exp, scale=0.125),
                        writes=[PSB(bank), ("PT", pbuf, slot)])

        def A_pv(ti, kv):
            tix = ti + 1
            pbuf = ti % 2
            bank = nbank()

            def mm(e):
                for j in range(4):
                    for ch in (0, 1):
                        ins = e.matmul(ps[bank][:, j * 65:(j + 1) * 65],
                                       lhsT=PT[:, pbuf, kv * 2 + ch, j * 128:(j + 1) * 128],
                                       rhs=vaug[:, tix - 1 + ch, kv, 0:65],
                                       start=(ch == 0), stop=(ch == 1))
                return ins
            S.add("pe", mm, reads=[("PT", pbuf, kv * 2), ("PT", pbuf, kv * 2 + 1), ("vaug", tix - 1), ("vaug", tix)],
                  writes=[PSB(bank)])
            pv = ps[bank][:, 0:260].rearrange("p (j d) -> p j d", d=65)
            dsl = den[:, pbuf, kv * 4:(kv + 1) * 4]
            S.add("dve", lambda e: e.scalar_tensor_tensor(
                out=dsl.unsqueeze(2), in0=pv[:, :, 64:65], scalar=2.0,
                in1=esink2[:, kv * 4:(kv + 1) * 4].unsqueeze(2), op0=ALU.mult, op1=ALU.add),
                reads=["esink2"], writes=[PSB(bank), ("den", pbuf, kv)])
            S.add("dve", lambda e: e.reciprocal(out=dsl, in_=dsl), writes=[("den", pbuf, kv)])
            S.add("dve", lambda e: e.tensor_tensor(
                out=a_tm[:, pbuf, kv * 256:(kv + 1) * 256].rearrange("p (j d) -> p j d", d=64),
                in0=pv[:, :, 0:64], in1=dsl.unsqueeze(2).to_broadcast([128, 4, 64]), op=ALU.mult),
                reads=[("den", pbuf, kv)], writes=[PSB(bank), ("a_tm", pbuf, kv)])

        def A_tr(ti):
            gb, k = (ti // 4) % 2, ti % 4
            pbuf = ti % 2
            pst = nbank()
            pb = psbf(pst)[:, 0:512].rearrange("p (c t) -> p c t", c=4)

            def tr(e):
                for c in range(4):
                    ins = e.transpose(out=pb[:, c, :], in_=a_tm[:, pbuf, c * 128:(c + 1) * 128], identity=identb[:])
                return ins
            S.add("pe", tr, reads=[("a_tm", pbuf, 0), ("a_tm", pbuf, 1), "identb"], writes=[PSB(pst)])
            S.add("dve", lambda e: e.tensor_tensor(out=mixT[:, gb, 0:4, k * 128:(k + 1) * 128], in0=pb,
                                                   in1=sga[:, :, k * 128:(k + 1) * 128], op=ALU.mult),
                  reads=[("sga", c) for c in range(4)], writes=[PSB(pst), ("mixA", gb, k)])

        def P_mm(G, g):
            gb = G % 2
            psb = nbank()
            S.add("pe", lambda e: e.matmul(ps[psb][:, :], lhsT=wpoolb[:, g, :], rhs=dT[:, gb, g, :],
                                           start=True, stop=True),
                  reads=[("dT", gb, k) for k in range(4)] + ["wpoolb"], writes=[PSB(psb)])
            S.add("dve", lambda e: e.scalar_tensor_tensor(
                out=mixT[:, gb, 4 + g, :], in0=ps[psb][:, :], scalar=psc05[:, g:g + 1], in1=sgp[:, g, :],
                op0=ALU.mult, op1=ALU.mult),
                reads=["psc05", ("sgp", g)], writes=[PSB(psb), ("mixP", gb, g)])

        def O_half(ti, n):
            gb, k = (ti // 4) % 2, ti % 4
            sl = ti % XR
            psy = nbank()

            def mm(e):
                for c in range(8):
                    ins = e.matmul(ps[psy][:, :], lhsT=mixT[:, gb, c, k * 128:(k + 1) * 128],
                                   rhs=woutb[:, c, n * 512:(n + 1) * 512], start=(c == 0), stop=(c == 7))
                return ins
            S.add("pe", mm, reads=[("mixA", gb, k)] + [("mixP", gb, g) for g in range(4)] + WOUT[n],
                  writes=[PSB(psy)])
            S.add("dve", lambda e: e.tensor_tensor(out=xr[:, sl, n * 512:(n + 1) * 512], in0=ps[psy][:, :],
                                                   in1=xr[:, sl, n * 512:(n + 1) * 512], op=ALU.add),
                  writes=[PSB(psy), ("xr", sl, n)])

        def O_tail_a(ti):
            sl = ti % XR
            col = 20 + ti
            yk = [("xr", sl, 0), ("xr", sl, 1)]
            S.add("act", lambda e: e.activation(out=junk[:], in_=xr[:, sl, :], func=AF.Square, scale=1.0 / 32,
                                                accum_out=stat[:, col:col + 1]),
                  reads=yk, writes=["junk", ("stat", col)])
            S.add("pool", lambda e: e.tensor_scalar(out=stat[:, col:col + 1], in0=stat[:, col:col + 1],
                                                    scalar1=float(EPS), scalar2=None, op0=ALU.add),
                  writes=[("stat", col)])
            S.add("pool", lambda e: e.tensor_tensor(out=stat[:, col:col + 1], in0=stat[:, col:col + 1],
                                                    in1=stat[:, 0:1], op=ALU.pow),
                  reads=["c_m05"], writes=[("stat", col)])

        def O_tail_b(ti):
            sl = ti % XR
            col = 20 + ti
            yk = [("xr", sl, 0), ("xr", sl, 1)]
            S.add("dve", lambda e: e.scalar_tensor_tensor(out=xr[:, sl, :], in0=xr[:, sl, :],
                                                          scalar=stat[:, col:col + 1], in1=gfr[:],
                                                          op0=ALU.mult, op1=ALU.mult),
                  reads=[("stat", col), "gfr"], writes=yk)
            dma("sp", ("y", sl), out_d[ti * 128:(ti + 1) * 128, :], xr[:, sl, :],
                reads=yk, writes=[("out", ti)])

        def late_setup():
            setup_late()
            S.add("act", lambda e: e.activation(out=esink2[:], in_=esink2[:], func=AF.Exp), writes=["esink2"])
            S.add("dve", lambda e: e.tensor_scalar(out=esink2[:], in0=esink2[:], scalar1=2.0, scalar2=None,
                                                   op0=ALU.mult), writes=["esink2"])

        pending = []

        def tick():
            for p in pending:
                p[0] -= 1
            while pending and pending[0][0] <= 0:
                pending.pop(0)[1]()

        def later(n, fn):
            pending.append([n, fn])

        def pre(ti, delay=2):
            if ti >= NT:
                return
            F_pre_a(ti)

            def fb():
                F_pre_b(ti)
                if XS - 1 <= ti + XS - 1 < NT:
                    load_xs(ti + XS - 1)
            later(delay, fb)

        def otail(ti, delay=2):
            O_tail_a(ti)

            def fb():
                O_tail_b(ti)
                if ti + XR < NT:
                    load_xr(ti + XR)
            later(delay, fb)

        def flush():
            while pending:
                pending.pop(0)[1]()

        def preb(ti):
            F_pre_b(ti)
            if XS - 1 <= ti + XS - 1 < NT:
                load_xs(ti + XS - 1)
        wdma("wq", C_Q, C_Q + 512)
        F_pre_a(-1); F_pre_a(0)
        setup_rope()
        F_pre_a(1)
        preb(-1); preb(0)
        wdma("wga", C_GA, C_GA + 512)
        F_pre_a(2)
        F_tr(-1); F_tr(0)
        preb(1)
        F_mm(-1)
        F_tr(1)
        F_pre_a(3)
        preb(2)
        F_mm(0); F_tr2(-1)
        F_tr(2)
        F_pre_a(4)
        dma_tables()
        wdma("wu", C_U, C_U + 512)
        wdma("wgp", C_GP, C_GP + 512)
        dma_wpool()
        preb(3)
        F_mm(1); F_tr2(0)
        F_tr(3)
        preb(4)
        F_mm(2); F_tr2(1)
        late_setup()
        F_mm(3); F_tr2(2)
        F_tr(4)
        for ti in range(XR):
            load_xr(ti)

        atr_done = set()
        pmm_done = {}

        for G in range(NG):
            has_f = G + 1 < NG
            has_o = G >= 1
            last = G == NG - 1
            at = [4 * G + k for k in range(4)]
            ft = [4 * (G + 1) + k for k in range(4)]
            ot = [4 * (G - 1) + k for k in range(4)]
            heavy = [("ga", 0), ("ga", 1), ("ga", 2), ("ga", 3)]
            fd3 = {"done": G != 0}
            gp_wait = []
            if G == 0:
                heavy += [("fu", -1), ("fu", 0), ("gp", 0), ("fu", 1), ("fd", 0), ("gp", 1), ("fu", 2), ("fd", 1),
                          ("gp", 2), ("fu", 3), ("fd", 2), ("gp", 3), ("fd", 3), ("fu", 4), ("fu", 5), ("fd", 4),
                          ("fu", 6), ("fd", 5), ("fu", 7), ("fd", 6)]
            elif last:
                heavy += [("gp", 0), ("gp", 1), ("gp", 2), ("gp", 3)]
                for k in range(4):
                    heavy += [("oh", ot[k], 0), ("oh", ot[k], 1)]
                for k in range(4):
                    heavy += [("oh", at[k], 0), ("oh", at[k], 1)]
            elif has_o:
                for k in range(4):
                    heavy += [("oh", ot[k], 0), ("oh", ot[k], 1), ("gp", k)]
            else:
                heavy += [("gp", 0), ("gp", 1), ("gp", 2), ("gp", 3)]
            pm_ready = []
            pmm_done[G] = 0

            def ready(h):
                if h[0] != "oh":
                    return True
                ti = h[1]
                if ti // 4 == G:
                    return ti in atr_done and pmm_done[G] == 4
                return True

            def take(do_tick=False):
                if do_tick:
                    tick()
                if not heavy or not ready(heavy[0]):
                    return
                h = heavy.pop(0)
                if h[0] == "oh":
                    O_half(h[1], h[2])
                    if h[2] == 1:
                        otail(h[1], delay=1)
                elif h[0] == "ga":
                    M_chunk(G, h[1])
                elif h[0] == "fu":
                    F_u(h[1])
                elif h[0] == "fd":
                    F_D(h[1])
                    if h[1] == 3:
                        fd3["done"] = True
                        pm_ready.extend(gp_wait)
                        del gp_wait[:]
                else:
                    M_chunk(G, 4 + h[1])
                    (pm_ready if fd3["done"] else gp_wait).append(h[1])

            def pmm():
                while pm_ready:
                    P_mm(G, pm_ready.pop(0))
                    pmm_done[G] += 1

            for i in range(4):
                nxt = ft[i] + 1
                if has_f and nxt < NT:
                    pre(nxt, delay=1)
                A_qk(at[i])
                take()
                if has_f and G > 0:
                    F_u(ft[i])
                A_pv(at[i], 0)
                take()
                A_pv(at[i], 1)
                if i == 0:
                    take()
                    take()
                else:
                    take()
                A_tr(at[i])
                atr_done.add(at[i])
                tick()
                if has_f:
                    F_mm(ft[i])
                take()
                if i == 0:
                    F_tr2(at[3])
                    if G > 0:
                        F_D(at[3])
                elif has_f:
                    F_tr2(ft[i - 1])
                    if G > 0:
                        F_D(ft[i - 1])
                take(True)
                if has_f and nxt < NT:
                    F_tr(nxt)
                take(True)
                pmm()
                if G == 0:
                    dma_wout(2 * i, 2 * i + 2)
                if G == 0 and i == 0:
                    dma("sp", "gf", gfr[:], gf_d, writes=["gfr"])
            guard = 0
            while heavy and guard < 100:
                take(True)
                guard += 1
            assert not heavy
            pmm()
            flush()
        S.add("sp", lambda e: e.nop(), reads=[("out", ti) for ti in range(NT)])

        S.emit(nc, es)
    return nc


_NC_CACHE = {}


def _host_tables(seg):
    half = 32
    inv_freq = np.array([
        0x3f800000, 0x3f3ff911, 0x3f0ff59a, 0x3ed7e89b, 0x3ea1e89b, 0x3e72d423, 0x3e361887, 0x3e088d77,
        0x3dcccccd, 0x3d99940d, 0x3d6655c2, 0x3d2cba15, 0x3d0186e3, 0x3cc2434f, 0x3c91ad39, 0x3c5a7bf2,
        0x3c23d70a, 0x3bf5b9b0, 0x3bb8449c, 0x3b8a2e77, 0x3b4f3e38, 0x3b1b690d, 0x3ae91528, 0x3aaec98e,
        0x3a83126f, 0x3a44948c, 0x3a136a16, 0x39dd1725, 0x39a5cb60, 0x3978a815, 0x393a7753, 0x390bd472,
    ], dtype=np.uint32).view(np.float32)
    pos = (seg * T - 128 + np.arange((NT + 1) * 128)).astype(np.float32)
    ang = (pos[:, None] * inv_freq[None, :]).astype(np.float32)
    cos_t = np.cos(ang).astype(np.float32)
    sin_t = np.sin(ang).astype(np.float32)
    cos_t = np.ascontiguousarray(cos_t.reshape(NT + 1, 128, 32).transpose(1, 0, 2).reshape(128, (NT + 1) * 32))
    sin_t = np.ascontiguousarray(sin_t.reshape(NT + 1, 128, 32).transpose(1, 0, 2).reshape(128, (NT + 1) * 32))
    j = np.arange(128)[:, None]
    i = np.arange(128)[None, :]
    NEG = np.float32(-30000.0)
    m_cur = np.where(j <= i, np.float32(0.0), NEG).astype(np.float32)
    m_prev = np.where(j > i, np.float32(0.0), NEG).astype(np.float32)
    m_prev0 = m_prev if seg != 0 else np.full((128, 128), NEG, np.float32)
    masks = np.stack([m_prev0, m_prev, m_cur], axis=1).astype(np.float32)
    atab = np.empty((128, 3, 2, 128), np.float32)
    for p in range(128):
        for h in range(2):
            atab[p, :, h, :] = masks[:, :, (p % 64) + 64 * h].T
    btab = (np.arange(128)[:, None] % 64 == np.arange(64)[None, :]).astype(np.float32)
    tp = np.arange(128)[:, None]
    tq = np.arange(128)[None, :]
    band = np.zeros((128, 12, 128), np.float32)
    for g, w in enumerate(WINDOWS):
        inwin = ((tq - tp) >= 0) & ((tq - tp) < w)
        cur = np.where(inwin, np.float32(1.0 / w), np.float32(0.0)) - (tp == tq).astype(np.float32)
        prev = np.where((tq + 128 - tp) < w, np.float32(1.0 / w), np.float32(0.0))
        band[:, 2 * g, :] = cur
        band[:, 2 * g + 1, :] = prev
        if seg == 0:
            cnt = np.minimum(tq + 1, w).astype(np.float32)
            cur0 = np.where(inwin, np.float32(1.0) / cnt, np.float32(0.0)) - (tp == tq).astype(np.float32)
        else:
            cur0 = cur
        band[:, 8 + g, :] = cur0
    return cos_t, sin_t, (atab, btab), band


def kernel(x, norm_in, w_in, attn_sinks, w_pool, pool_scale, w_out, norm_final):
    x = np.asarray(x, np.float32)
    w_in0 = np.asarray(w_in, np.float32)[0]
    qperm = []
    for j in range(4):
        for kv in range(2):
            h = kv * 4 + j
            qperm.extend(range(h * 64, (h + 1) * 64))
    cols = np.concatenate([np.array(qperm), np.arange(512, 768), np.arange(1280, 1792), np.arange(768, 1280),
                           np.arange(1792, DIN)])
    w_in_p = np.ascontiguousarray(w_in0[:, cols])
    w_out0 = np.ascontiguousarray(np.asarray(w_out, np.float32)[0])
    w_pool0 = np.ascontiguousarray(np.asarray(w_pool, np.float32)[0])
    g_row = np.ascontiguousarray(np.asarray(norm_in, np.float32)[0][None, :])
    gf_row = np.ascontiguousarray(np.asarray(norm_final, np.float32)[None, :])
    sink_rep = np.ascontiguousarray(np.broadcast_to(np.asarray(attn_sinks, np.float32)[0][None, :], (128, 8)))
    pscale = np.ascontiguousarray(np.asarray(pool_scale, np.float32)[0].reshape(4, 128).T)
    ident = np.eye(128, dtype=np.float32)

    in_maps = []
    for c in range(N_CORES):
        b, seg = c // 4, c % 4
        xc = np.ascontiguousarray(x[b, seg * T:(seg + 1) * T, :])
        if seg == 0:
            xh = np.zeros((128, D), np.float32)
        else:
            xh = np.ascontiguousarray(x[b, seg * T - 128:seg * T, :])
        cos_t, sin_t, masks, band = _host_tables(seg)
        in_maps.append({
            "x": xc, "xh": xh, "w_in": w_in_p, "w_out": w_out0, "w_pool": w_pool0,
            "g_row": g_row, "gf_row": gf_row, "sink_rep": sink_rep, "pscale": pscale,
            "cos_t": cos_t, "sin_t": sin_t, "atab": masks[0], "btab": masks[1], "band": band, "ident": ident,
        })
    if "nc" not in _NC_CACHE:
        _NC_CACHE["nc"] = build_nc()
    nc = _NC_CACHE["nc"]
    res = run_bass_kernel_spmd(nc, in_maps, core_ids=list(range(N_CORES)))
    out = np.empty((BATCH, SEQ, D), np.float32)
    for c in range(N_CORES):
        b, seg = c // 4, c % 4
        out[b, seg * T:(seg + 1) * T, :] = res.results[c]["out"]
    return out
```

```python
import math
from contextlib import ExitStack

import numpy as np
import concourse.bass as bass
import concourse.mybir as mybir
from concourse.bass_utils import run_bass_kernel_spmd

F32 = mybir.dt.float32
BF16 = mybir.dt.bfloat16
ALU = mybir.AluOpType
AF = mybir.ActivationFunctionType

N_CORES = 8
D = 1024
SEQ = 8192
BATCH = 2
T = 2048
NT = T // 128
NG = 4
DIN = 2304
C_Q, C_K, C_V, C_U, C_GA, C_GP = 0, 512, 640, 768, 1280, 1792
EPS = 1e-5
XSLOTS = 8
VW = 80
WINDOWS = (2, 4, 8, 16)


class Op:
    __slots__ = ("eng", "fn", "deps", "dma", "sig", "idx")


class Sched:
    def __init__(self):
        self.ops = []
        self.last_w = {}
        self.readers = {}

    def add(self, eng, fn, reads=(), writes=(), dma=None):
        op = Op()
        op.eng, op.fn, op.dma, op.sig, op.idx = eng, fn, dma, None, len(self.ops)
        deps = set()
        for r in reads:
            if r not in self.last_w:
                raise RuntimeError("read of never-written resource %r" % (r,))
            deps.add(self.last_w[r])
        for w in writes:
            if w in self.last_w:
                deps.add(self.last_w[w])
            for rd in self.readers.get(w, ()):
                deps.add(rd)
        for r in reads:
            self.readers.setdefault(r, []).append(op.idx)
        for w in writes:
            self.last_w[w] = op.idx
            self.readers[w] = []
        op.deps = sorted(deps)
        self.ops.append(op)
        return op.idx

    def emit(self, nc, es):
        ops = self.ops
        needed = [False] * len(ops)
        for op in ops:
            for d in op.deps:
                needed[d] = True
        cnt = {}
        sem_keys = []
        for op in ops:
            if op.dma is not None:
                k = ("dma", op.dma)
            elif needed[op.idx]:
                k = ("eng", op.eng)
            else:
                continue
            if k not in cnt:
                cnt[k] = 0
                sem_keys.append(k)
            cnt[k] += 16 if op.dma is not None else 1
            op.sig = (k, cnt[k])
        sems = {}
        for i, k in enumerate(sem_keys):
            sems[k] = es.enter_context(nc.semaphore("s%d" % i))
        block = es.enter_context(nc.Block())
        by_eng = {}
        for op in ops:
            by_eng.setdefault(op.eng, []).append(op)

        def run(eng_name, eng):
            known = {}
            for op in by_eng.get(eng_name, []):
                want = {}
                for d in op.deps:
                    dop = ops[d]
                    if dop.dma is None and dop.eng == eng_name and eng_name == "pe":
                        continue
                    k, v = dop.sig
                    if v > want.get(k, 0):
                        want[k] = v
                for k, v in want.items():
                    if known.get(k, 0) >= v:
                        continue
                    eng.wait_ge(sems[k], v)
                    known[k] = v
                ins = op.fn(eng)
                if op.sig is not None:
                    k, v = op.sig
                    ins.then_inc(sems[k], 16 if op.dma is not None else 1)

        @block.sync
        def _(e):
            run("sp", e)

        @block.scalar
        def _(e):
            run("act", e)

        @block.vector
        def _(e):
            run("dve", e)

        @block.gpsimd
        def _(e):
            run("pool", e)

        @block.tensor
        def _(e):
            run("pe", e)


def build_nc(debug=False):
    nc = bass.Bass("TRN2", target_bir_lowering=False)

    def din(name, shape):
        return nc.dram_tensor(name, list(shape), F32, kind="ExternalInput").ap()

    x_d = din("x", [T, D])
    xh_d = din("xh", [128, D])
    win_d = din("w_in", [D, DIN])
    wout_d = din("w_out", [D, D])
    wpool_d = din("w_pool", [4, 128, 128])
    g_d = din("g_row", [1, D]).partition_broadcast(128)
    gf_d = din("gf_row", [1, D]).partition_broadcast(128)
    sink_d = din("sink_rep", [128, 8])
    psc_d = din("pscale", [128, 4])
    cos_d = din("cos_t", [128, (NT + 1) * 32])
    sin_d = din("sin_t", [128, (NT + 1) * 32])
    atab_d = din("atab", [128, 3, 2, 128])
    btab_d = din("btab", [128, 64])
    band_d = din("band", [128, 12, 128])
    ident_d = din("ident", [128, 128])
    out_d = nc.dram_tensor("out", [T, D], F32, kind="ExternalOutput").ap()

    S = Sched()
    with ExitStack() as es:
        E = es.enter_context

        def sb(name, shape, dt):
            return E(nc.sbuf_tensor("sb_" + name, list(shape), dt))

        XS, XR = 4, 4
        xs = sb("xs", [128, XS, D], F32)
        xr = sb("xr", [128, XR, D], F32)
        junk = sb("junk", [128, D], BF16)
        hbf = sb("hbf", [128, 2, D], BF16)
        hT = sb("hT", [128, 2, 8, 512], BF16)
        hTh = sb("hTh", [128, 8, 128], BF16)
        winb = sb("winb", [128, 8, DIN], BF16)
        woutb = sb("woutb", [128, 8, D], BF16)
        wpoolb = sb("wpoolb", [128, 4, 128], BF16)
        qk_sb = sb("qk_sb", [128, 640], F32)
        ropeA = sb("ropeA", [128, 640], F32)
        ropeB = sb("ropeB", [128, 640], F32)
        qrot = sb("qrot", [128, 2, 640], BF16)
        qT = sb("qT", [128, 2, 4, 512], BF16)
        kT = sb("kT", [128, (NT + 1) * 128], BF16)
        vaug = sb("vaug", [128, NT + 1, 2, VW], BF16)
        tsb = sb("tsb", [128, 2, 512], F32)
        sga = sb("sga", [128, 4, 512], BF16)
        sgp = sb("sgp", [128, 4, 512], BF16)
        u_tm = sb("u_tm", [128, 3, 512], BF16)
        bandb = sb("bandb", [128, 12, 128], BF16)
        dT = sb("dT", [128, 2, 4, 512], BF16)
        PT = sb("PT", [128, 2, 4, 512], BF16)
        a_tm = sb("a_tm", [128, 2, 512], BF16)
        den = sb("den", [128, 2, 8], F32)
        mixT = sb("mixT", [128, 2, 8, 512], BF16)
        cs = sb("cs", [128, NT + 1, 32], F32)
        sn = sb("sn", [128, NT + 1, 32], F32)
        nsn = sb("nsn", [128, NT + 1, 32], F32)
        atab = sb("atab", [128, 3, 2, 128], BF16)
        btab = sb("btab", [128, 64], BF16)
        g32 = sb("g32", [128, D], F32)
        gfr = sb("gfr", [128, D], F32)
        identb = sb("identb", [128, 128], BF16)
        esink2 = sb("esink2", [128, 8], F32)
        psc05 = sb("psc05", [128, 4], F32)
        stat = sb("stat", [128, 64], F32)

        ps = [E(nc.psum_tensor("ps%d" % i, [128, 512], F32)) for i in range(8)]
        rot = {"i": 0}

        def nbank():
            b = rot["i"] % 8
            rot["i"] += 1
            return b

        def PSB(i):
            return ("psum", i)

        def psbf(i):
            return ps[i][:].bitcast(BF16)

        def dma(eng, key, out, in_, reads=(), writes=()):
            S.add(eng, lambda e: e.dma_start(out=out, in_=in_), reads=reads, writes=writes, dma=key)

        def load_xs(ti):
            sl = (ti + 1) % XS
            src = xh_d if ti < 0 else x_d[ti * 128:(ti + 1) * 128, :]
            dma("sp", ("xs", sl), xs[:, sl, :], src, writes=[("xs", sl)])

        def load_xr(ti):
            sl = ti % XR
            dma("sp", ("xr", sl), xr[:, sl, :], x_d[ti * 128:(ti + 1) * 128, :],
                writes=[("xr", sl, 0), ("xr", sl, 1)])

        load_xs(-1)
        dma("sp", "g", g32[:], g_d, writes=["g32"])
        load_xs(0)
        dma("sp", "cos", cs[:], cos_d.rearrange("p (t d) -> p t d", d=32), writes=["cs"])
        dma("sp", "sin", sn[:], sin_d.rearrange("p (t d) -> p t d", d=32), writes=["sn"])
        load_xs(1)
        load_xs(2)
        win_v = win_d.rearrange("(c p) n -> p c n", p=128)
        wout_v = wout_d.rearrange("(c p) n -> p c n", p=128)

        def wkeys(tag):
            return [(tag, c) for c in range(8)]

        def wdma(tag, lo, hi):
            dma("pool", (tag, 0), winb[:, :, lo:hi], win_v[:, :, lo:hi], writes=[(tag, c) for c in range(8)])
        W_KV, W_Q, W_U, W_GA, W_GP = (wkeys(t_) for t_ in ("wkv", "wq", "wu", "wga", "wgp"))
        WOUT = [[("wout", c, n) for c in range(8)] for n in range(2)]
        WINB = [W_GA, W_GP]

        def dma_wpool():
            dma("pool", "wpool", wpoolb[:], wpool_d.rearrange("g c d -> c g d"), writes=["wpoolb"])

        def dma_wout(c0, c1):
            for c in range(c0, c1):
                dma("pool", ("wout", c), woutb[:, c, :], wout_v[:, c, :], writes=[("wout", c, 0), ("wout", c, 1)])

        def dma_tables():
            dma("pool", "atab", atab[:], atab_d, writes=["atab"])
            dma("pool", "btab", btab[:], btab_d, writes=["btab"])
            dma("pool", "band0", bandb[:, 0:8, :], band_d[:, 0:8, :], writes=["bandb"])
            dma("pool", "band1", bandb[:, 8:12, :], band_d[:, 8:12, :], writes=["bandb"])

        dma("pool", "ident", identb[:], ident_d, writes=["identb"])
        wdma("wkv", C_K, C_K + 256)

        dma("sp", "sink", esink2[:], sink_d, writes=["esink2"])
        dma("sp", "psc", psc05[:], psc_d, writes=["psc05"])

        S.add("dve", lambda e: e.memset(stat[:, 0:1], -0.5), writes=["c_m05"])
        S.add("act", lambda e: e.activation(out=stat[:, 62:63], in_=stat[:, 0:1], func=AF.Exp),
              reads=["c_m05"], writes=[("stat", 62)])

        def setup_rope():
            S.add("dve", lambda e: e.tensor_scalar(out=nsn[:], in0=sn[:], scalar1=-1.0, scalar2=None, op0=ALU.mult),
                  reads=["sn"], writes=["nsn"])
            S.add("pool", lambda e: e.memset(vaug[:], 1.0), writes=[("vaug", t) for t in range(NT + 1)])

        def setup_late():
            S.add("dve", lambda e: e.tensor_scalar(out=psc05[:], in0=psc05[:], scalar1=0.5, scalar2=None,
                                                   op0=ALU.mult), writes=["psc05"])

        def hT_dst(ti):
            if ti < 0:
                return hTh[:, :, :], ("hTh",)
            gb, k = (ti // 4) % 2, ti % 4
            return hT[:, gb, :, k * 128:(k + 1) * 128], ("hT", gb, k)

        def F_pre_a(ti):
            sl = (ti + 1) % XS
            col = 1 + (ti + 1)
            xk = ("xs", sl)
            S.add("act", lambda e: e.activation(out=junk[:], in_=xs[:, sl, :], func=AF.Square, scale=1.0 / 32,
                                                accum_out=stat[:, col:col + 1]),
                  reads=[xk], writes=["junk", ("stat", col)])
            S.add("pool", lambda e: e.tensor_scalar(out=stat[:, col:col + 1], in0=stat[:, col:col + 1],
                                                    scalar1=float(EPS), scalar2=None, op0=ALU.add),
                  writes=[("stat", col)])
            S.add("pool", lambda e: e.tensor_tensor(out=stat[:, col:col + 1], in0=stat[:, col:col + 1],
                                                    in1=stat[:, 0:1], op=ALU.pow),
                  reads=["c_m05"], writes=[("stat", col)])

        def F_pre_b(ti):
            sl = (ti + 1) % XS
            col = 1 + (ti + 1)
            hb = (ti + 1) % 2
            S.add("dve", lambda e: e.scalar_tensor_tensor(out=hbf[:, hb, :], in0=xs[:, sl, :],
                                                          scalar=stat[:, col:col + 1], in1=g32[:],
                                                          op0=ALU.mult, op1=ALU.mult),
                  reads=[("xs", sl), ("stat", col), "g32"], writes=[("hbf", hb)])

        def F_tr(ti):
            hb = (ti + 1) % 2
            psb = nbank()
            pb = psbf(psb).rearrange("p (c t) -> p c t", c=8)

            def tr(e):
                for c in range(8):
                    ins = e.transpose(out=pb[:, c, :], in_=hbf[:, hb, c * 128:(c + 1) * 128], identity=identb[:])
                return ins
            S.add("pe", tr, reads=[("hbf", hb), "identb"], writes=[PSB(psb)])
            dst, dkey = hT_dst(ti)
            S.add("act", lambda e: e.activation(out=dst, in_=pb, func=AF.Copy), writes=[PSB(psb), dkey])

        def F_mm(ti):
            halo = ti < 0
            src, skey = hT_dst(ti)
            tix = ti + 1
            qb = tix % 2
            psq = None if halo else nbank()
            pskv = nbank()

            def mm(e):
                ins = None
                for c in range(8):
                    if not halo:
                        e.matmul(ps[psq][:, :], lhsT=src[:, c, :], rhs=winb[:, c, C_Q:C_Q + 512],
                                 start=(c == 0), stop=(c == 7))
                    ins = e.matmul(ps[pskv][:, 0:256], lhsT=src[:, c, :], rhs=winb[:, c, C_K:C_K + 256],
                                   start=(c == 0), stop=(c == 7))
                return ins
            S.add("pe", mm, reads=[skey] + W_KV + ([] if halo else W_Q), writes=[PSB(pskv)] + ([] if halo else [PSB(psq)]))
            if not halo:
                S.add("act", lambda e: e.activation(out=qk_sb[:, 0:512], in_=ps[psq][:, :], func=AF.Copy),
                      writes=[PSB(psq), "qk_q"])
            S.add("act", lambda e: e.activation(out=qk_sb[:, 512:640], in_=ps[pskv][:, 0:128], func=AF.Copy),
                  writes=[PSB(pskv), "qk_k"])
            S.add("act", lambda e: e.activation(out=vaug[:, tix, :, 0:64],
                                                in_=ps[pskv][:, 128:256].rearrange("p (k d) -> p k d", k=2),
                                                func=AF.Copy),
                  writes=[PSB(pskv), ("vaug", tix)])
            h0 = 8 if halo else 0
            nh = 10 - h0
            c0 = h0 * 64

            def v4(t):
                return t[:, c0:640].rearrange("p (h t d) -> p h t d", t=2, d=32)

            def bc(tab, n2):
                a = tab[:, tix, :].unsqueeze(1)
                if n2:
                    a = a.unsqueeze(1)
                    return a.to_broadcast([128, nh, 2, 32])
                return a.to_broadcast([128, nh, 32])
            qkr = ["qk_k"] if halo else ["qk_q", "qk_k"]
            S.add("dve", lambda e: e.tensor_tensor(out=v4(ropeA), in0=v4(qk_sb), in1=bc(cs, True), op=ALU.mult),
                  reads=qkr + ["cs"], writes=["ropeA"])
            S.add("dve", lambda e: e.tensor_tensor(out=v4(ropeB)[:, :, 0, :], in0=v4(qk_sb)[:, :, 1, :],
                                                    in1=bc(nsn, False), op=ALU.mult),
                  reads=qkr + ["nsn"], writes=["ropeB0"])
            S.add("dve", lambda e: e.tensor_tensor(out=v4(ropeB)[:, :, 1, :], in0=v4(qk_sb)[:, :, 0, :],
                                                    in1=bc(sn, False), op=ALU.mult),
                  reads=qkr + ["sn"], writes=["ropeB1"])
            S.add("dve", lambda e: e.tensor_tensor(out=qrot[:, qb, c0:640], in0=ropeA[:, c0:640],
                                                   in1=ropeB[:, c0:640], op=ALU.add),
                  reads=["ropeA", "ropeB0", "ropeB1"], writes=[("qrot", qb)])

        def F_u(ti):
            src, skey = hT_dst(ti)
            tix = ti + 1
            us = tix % 3
            psu = nbank()

            def mm(e):
                for c in range(8):
                    ins = e.matmul(ps[psu][:, :], lhsT=src[:, c, :], rhs=winb[:, c, C_U:C_U + 512],
                                   start=(c == 0), stop=(c == 7))
                return ins
            S.add("pe", mm, reads=[skey] + W_U, writes=[PSB(psu)])
            S.add("act", lambda e: e.activation(out=u_tm[:, us, :], in_=ps[psu][:, :], func=AF.Copy),
                  writes=[PSB(psu), ("utm", us)])

        def F_D(ti):
            gb, k = (ti // 4) % 2, ti % 4
            tix = ti + 1
            cur, prv = tix % 3, (tix - 1) % 3
            psd = nbank()

            def mm(e):
                for g in range(4):
                    bc_i = (8 + g) if ti == 0 else 2 * g
                    e.matmul(ps[psd][:, g * 128:(g + 1) * 128], lhsT=u_tm[:, cur, g * 128:(g + 1) * 128],
                             rhs=bandb[:, bc_i, :], start=True, stop=False)
                    ins = e.matmul(ps[psd][:, g * 128:g * 128 + 16], lhsT=u_tm[:, prv, g * 128:(g + 1) * 128],
                                   rhs=bandb[:, 2 * g + 1, 0:16], start=False, stop=True)
                return ins
            S.add("pe", mm, reads=[("utm", cur), ("utm", prv), "bandb"], writes=[PSB(psd)])
            S.add("act", lambda e: e.activation(out=dT[:, gb, :, k * 128:(k + 1) * 128],
                                                in_=ps[psd][:, :].rearrange("p (g t) -> p g t", g=4), func=AF.Copy),
                  writes=[PSB(psd), ("dT", gb, k)])

        def F_tr2(ti):
            halo = ti < 0
            tix = ti + 1
            qb = tix % 2
            pst = nbank()
            pb = psbf(pst)[:, 0:640].rearrange("p (c t) -> p c t", c=5)
            cl = [4] if halo else [0, 1, 2, 3, 4]

            def tr(e):
                for c in cl:
                    ins = e.transpose(out=pb[:, c, :], in_=qrot[:, qb, c * 128:(c + 1) * 128], identity=identb[:])
                return ins
            S.add("pe", tr, reads=[("qrot", qb), "identb"], writes=[PSB(pst)])
            if not halo:
                gb, k = (ti // 4) % 2, ti % 4
                S.add("act", lambda e: e.activation(out=qT[:, gb, :, k * 128:(k + 1) * 128], in_=pb[:, 0:4, :],
                                                    func=AF.Copy),
                      writes=[PSB(pst), ("qT", gb, k)])
            S.add("act", lambda e: e.activation(out=kT[:, tix * 128:(tix + 1) * 128], in_=pb[:, 4, :], func=AF.Copy),
                  writes=[PSB(pst), ("kT", tix)])

        def M_chunk(G, cc):
            gb = G % 2
            col = C_GA + cc * 128
            psb = nbank()
            tb = cc % 2

            def mm(e):
                for c in range(8):
                    ins = e.matmul(ps[psb][:, :], lhsT=winb[:, c, col:col + 128], rhs=hT[:, gb, c, :],
                                   start=(c == 0), stop=(c == 7))
                return ins
            S.add("pe", mm, reads=[("hT", gb, k) for k in range(4)] + WINB[cc // 4], writes=[PSB(psb)])
            dstt, dkey = (sga, ("sga", cc)) if cc < 4 else (sgp, ("sgp", cc - 4))
            ci = cc % 4
            S.add("act", lambda e: e.activation(out=tsb[:, tb, :], in_=ps[psb][:, :], func=AF.Tanh, scale=0.5),
                  writes=[PSB(psb), ("tsb", tb)])
            S.add("dve", lambda e: e.scalar_tensor_tensor(out=dstt[:, ci, :], in0=tsb[:, tb, :], scalar=1.0,
                                                          in1=ps[psb][:, :], op0=ALU.add, op1=ALU.mult),
                  reads=[("tsb", tb)], writes=[PSB(psb), dkey])

        def A_qk(ti):
            b4 = [nbank() for _ in range(4)]
            SC_BANKS = {(0, 0): b4[0], (1, 0): b4[1], (0, 1): b4[2], (1, 1): b4[3]}
            gb, k = (ti // 4) % 2, ti % 4
            tix = ti + 1
            pbuf = ti % 2
            for ch in (0, 1):
                mi = 2 if ch == 1 else (0 if ti == 0 else 1)
                for kv in (0, 1):
                    bank = SC_BANKS[(kv, ch)]
                    ktile = tix - 1 + ch
                    S.add("pe", lambda e, kv=kv, bank=bank, ktile=ktile: e.matmul(
                        ps[bank][:, :].rearrange("p (j q) -> p j q", j=4),
                        lhsT=kT[kv * 64:(kv + 1) * 64, ktile * 128:(ktile + 1) * 128],
                        rhs=qT[kv * 64:(kv + 1) * 64, gb, :, k * 128:(k + 1) * 128],
                        start=True, stop=False),
                        reads=[("kT", ktile), ("qT", gb, k)], writes=[PSB(bank)])
                def mk(e, ch=ch, mi=mi):
                    ins = None
                    for h in (0, 1):
                        for g_ in range(4):
                            for kv in (0, 1):
                                bank = SC_BANKS[(kv, ch)]
                                c0 = g_ * 128 + h * 64
                                ins = e.matmul(ps[bank][:, c0:c0 + 64],
                                               lhsT=atab[kv * 64:(kv + 1) * 64, mi, h, :],
                                               rhs=btab[kv * 64:(kv + 1) * 64, :],
                                               start=False, stop=(h == 1 and g_ == 3))
                    return ins
                S.add("pe", mk, reads=["atab", "btab"], writes=[PSB(SC_BANKS[(0, ch)]), PSB(SC_BANKS[(1, ch)])])
            for kv in (0, 1):
                for ch in (0, 1):
                    bank = SC_BANKS[(kv, ch)]
                    slot = kv * 2 + ch
                    S.add("act", lambda e, bank=bank, slot=slot: e.activation(
                        out=PT[:, pbuf, slot, :], in_=ps[bank][:, :], func=AF.Exp, scale=0.125),
                        writes=[PSB(bank), ("PT", pbuf, slot)])

        def A_pv(ti, kv):
            tix = ti + 1
            pbuf = ti % 2
            bank = nbank()

            def mm(e):
                for j in range(4):
                    for ch in (0, 1):
                        ins = e.matmul(ps[bank][:, j * 65:(j + 1) * 65],
                                       lhsT=PT[:, pbuf, kv * 2 + ch, j * 128:(j + 1) * 128],
                                       rhs=vaug[:, tix - 1 + ch, kv, 0:65],
                                       start=(ch == 0), stop=(ch == 1))
                return ins
            S.add("pe", mm, reads=[("PT", pbuf, kv * 2), ("PT", pbuf, kv * 2 + 1), ("vaug", tix - 1), ("vaug", tix)],
                  writes=[PSB(bank)])
            pv = ps[bank][:, 0:260].rearrange("p (j d) -> p j d", d=65)
            dsl = den[:, pbuf, kv * 4:(kv + 1) * 4]
            S.add("dve", lambda e: e.scalar_tensor_tensor(
                out=dsl.unsqueeze(2), in0=pv[:, :, 64:65], scalar=2.0,
                in1=esink2[:, kv * 4:(kv + 1) * 4].unsqueeze(2), op0=ALU.mult, op1=ALU.add),
                reads=["esink2"], writes=[PSB(bank), ("den", pbuf, kv)])
            S.add("dve", lambda e: e.reciprocal(out=dsl, in_=dsl), writes=[("den", pbuf, kv)])
            S.add("dve", lambda e: e.tensor_tensor(
                out=a_tm[:, pbuf, kv * 256:(kv + 1) * 256].rearrange("p (j d) -> p j d", d=64),
                in0=pv[:, :, 0:64], in1=dsl.unsqueeze(2).to_broadcast([128, 4, 64]), op=ALU.mult),
                reads=[("den", pbuf, kv)], writes=[PSB(bank), ("a_tm", pbuf, kv)])

        def A_tr(ti):
            gb, k = (ti // 4) % 2, ti % 4
            pbuf = ti % 2
            pst = nbank()
            pb = psbf(pst)[:, 0:512].rearrange("p (c t) -> p c t", c=4)

            def tr(e):
                for c in range(4):
                    ins = e.transpose(out=pb[:, c, :], in_=a_tm[:, pbuf, c * 128:(c + 1) * 128], identity=identb[:])
                return ins
            S.add("pe", tr, reads=[("a_tm", pbuf, 0), ("a_tm", pbuf, 1), "identb"], writes=[PSB(pst)])
            S.add("dve", lambda e: e.tensor_tensor(out=mixT[:, gb, 0:4, k * 128:(k + 1) * 128], in0=pb,
                                                   in1=sga[:, :, k * 128:(k + 1) * 128], op=ALU.mult),
                  reads=[("sga", c) for c in range(4)], writes=[PSB(pst), ("mixA", gb, k)])

        def P_mm(G, g):
            gb = G % 2
            psb = nbank()
            S.add("pe", lambda e: e.matmul(ps[psb][:, :], lhsT=wpoolb[:, g, :], rhs=dT[:, gb, g, :],
                                           start=True, stop=True),
                  reads=[("dT", gb, k) for k in range(4)] + ["wpoolb"], writes=[PSB(psb)])
            S.add("dve", lambda e: e.scalar_tensor_tensor(
                out=mixT[:, gb, 4 + g, :], in0=ps[psb][:, :], scalar=psc05[:, g:g + 1], in1=sgp[:, g, :],
                op0=ALU.mult, op1=ALU.mult),
                reads=["psc05", ("sgp", g)], writes=[PSB(psb), ("mixP", gb, g)])

        def O_half(ti, n):
            gb, k = (ti // 4) % 2, ti % 4
            sl = ti % XR
            psy = nbank()

            def mm(e):
                for c in range(8):
                    ins = e.matmul(ps[psy][:, :], lhsT=mixT[:, gb, c, k * 128:(k + 1) * 128],
                                   rhs=woutb[:, c, n * 512:(n + 1) * 512], start=(c == 0), stop=(c == 7))
                return ins
            S.add("pe", mm, reads=[("mixA", gb, k)] + [("mixP", gb, g) for g in range(4)] + WOUT[n],
                  writes=[PSB(psy)])
            S.add("dve", lambda e: e.tensor_tensor(out=xr[:, sl, n * 512:(n + 1) * 512], in0=ps[psy][:, :],
                                                   in1=xr[:, sl, n * 512:(n + 1) * 512], op=ALU.add),
                  writes=[PSB(psy), ("xr", sl, n)])

        def O_tail_a(ti):
            sl = ti % XR
            col = 20 + ti
            yk = [("xr", sl, 0), ("xr", sl, 1)]
            S.add("act", lambda e: e.activation(out=junk[:], in_=xr[:, sl, :], func=AF.Square, scale=1.0 / 32,
                                                accum_out=stat[:, col:col + 1]),
                  reads=yk, writes=["junk", ("stat", col)])
            S.add("pool", lambda e: e.tensor_scalar(out=stat[:, col:col + 1], in0=stat[:, col:col + 1],
                                                    scalar1=float(EPS), scalar2=None, op0=ALU.add),
                  writes=[("stat", col)])
            S.add("pool", lambda e: e.tensor_tensor(out=stat[:, col:col + 1], in0=stat[:, col:col + 1],
                                                    in1=stat[:, 0:1], op=ALU.pow),
                  reads=["c_m05"], writes=[("stat", col)])

        def O_tail_b(ti):
            sl = ti % XR
            col = 20 + ti
            yk = [("xr", sl, 0), ("xr", sl, 1)]
            S.add("dve", lambda e: e.scalar_tensor_tensor(out=xr[:, sl, :], in0=xr[:, sl, :],
                                                          scalar=stat[:, col:col + 1], in1=gfr[:],
                                                          op0=ALU.mult, op1=ALU.mult),
                  reads=[("stat", col), "gfr"], writes=yk)
            dma("sp", ("y", sl), out_d[ti * 128:(ti + 1) * 128, :], xr[:, sl, :],
                reads=yk, writes=[("out", ti)])

        def O_last_stat(ti, n):
            sl = ti % XR
            S.add("act", lambda e: e.activation(out=junk[:, n * 512:(n + 1) * 512],
                                                in_=xr[:, sl, n * 512:(n + 1) * 512], func=AF.Square,
                                                scale=1.0 / 32, accum_out=stat[:, 40 + n:41 + n]),
                  reads=[("xr", sl, n)], writes=["junk", ("stat", 40 + n)])

        def O_last_a(ti):
            col = 20 + ti
            S.add("pool", lambda e: e.tensor_tensor(out=stat[:, col:col + 1], in0=stat[:, 40:41],
                                                    in1=stat[:, 41:42], op=ALU.add),
                  reads=[("stat", 40), ("stat", 41)], writes=[("stat", col)])
            S.add("pool", lambda e: e.tensor_scalar(out=stat[:, col:col + 1], in0=stat[:, col:col + 1],
                                                    scalar1=float(EPS), scalar2=None, op0=ALU.add),
                  writes=[("stat", col)])
            S.add("pool", lambda e: e.tensor_tensor(out=stat[:, col:col + 1], in0=stat[:, col:col + 1],
                                                    in1=stat[:, 0:1], op=ALU.pow),
                  reads=["c_m05"], writes=[("stat", col)])

        def O_last_b(ti):
            sl = ti % XR
            col = 20 + ti
            for n in (0, 1):
                S.add("dve", lambda e, n=n: e.scalar_tensor_tensor(
                    out=xr[:, sl, n * 512:(n + 1) * 512], in0=xr[:, sl, n * 512:(n + 1) * 512],
                    scalar=stat[:, col:col + 1], in1=gfr[:, n * 512:(n + 1) * 512],
                    op0=ALU.mult, op1=ALU.mult),
                    reads=[("stat", col), "gfr"], writes=[("xr", sl, n)])
                dma("sp", "gf" if n == 0 else ("y", sl),
                    out_d[ti * 128:(ti + 1) * 128, n * 512:(n + 1) * 512], xr[:, sl, n * 512:(n + 1) * 512],
                    reads=[("xr", sl, n)], writes=[("outh", ti) if n == 0 else ("out", ti)])

        def late_setup():
            setup_late()
            S.add("act", lambda e: e.activation(out=esink2[:], in_=esink2[:], func=AF.Exp), writes=["esink2"])
            S.add("dve", lambda e: e.tensor_scalar(out=esink2[:], in0=esink2[:], scalar1=2.0, scalar2=None,
                                                   op0=ALU.mult), writes=["esink2"])

        pending = []

        def tick():
            for p in pending:
                p[0] -= 1
            while pending and pending[0][0] <= 0:
                pending.pop(0)[1]()

        def later(n, fn):
            pending.append([n, fn])

        def pre(ti, delay=2):
            if ti >= NT:
                return
            F_pre_a(ti)

            def fb():
                F_pre_b(ti)
                if XS - 1 <= ti + XS - 1 < NT:
                    load_xs(ti + XS - 1)
            later(delay, fb)

        def otail(ti, delay=2):
            O_tail_a(ti)

            def fb():
                O_tail_b(ti)
                if ti + XR < NT:
                    load_xr(ti + XR)
            later(delay, fb)

        def flush():
            while pending:
                pending.pop(0)[1]()

        def preb(ti):
            F_pre_b(ti)
            if XS - 1 <= ti + XS - 1 < NT:
                load_xs(ti + XS - 1)
        wdma("wq", C_Q, C_Q + 512)
        F_pre_a(-1); F_pre_a(0)
        setup_rope()
        F_pre_a(1)
        preb(-1); preb(0)
        wdma("wga", C_GA, C_GA + 512)
        F_pre_a(2)
        F_tr(-1); F_tr(0)
        preb(1)
        F_mm(-1)
        F_tr(1)
        F_pre_a(3)
        preb(2)
        F_mm(0); F_tr2(-1)
        F_tr(2)
        F_pre_a(4)
        dma_tables()
        wdma("wu", C_U, C_U + 512)
        wdma("wgp", C_GP, C_GP + 512)
        dma_wpool()
        preb(3)
        F_mm(1); F_tr2(0)
        F_tr(3)
        preb(4)
        F_mm(2); F_tr2(1)
        late_setup()
        F_mm(3); F_tr2(2)
        F_tr(4)
        for ti in range(XR):
            load_xr(ti)

        atr_done = set()
        pmm_done = {}

        for G in range(NG):
            has_f = G + 1 < NG
            has_o = G >= 1
            last = G == NG - 1
            at = [4 * G + k for k in range(4)]
            ft = [4 * (G + 1) + k for k in range(4)]
            ot = [4 * (G - 1) + k for k in range(4)]
            heavy = [("ga", 0), ("ga", 1), ("ga", 2), ("ga", 3)]
            fd3 = {"done": G != 0}
            gp_wait = []
            if G == 0:
                heavy += [("fu", -1), ("fu", 0), ("gp", 0), ("fu", 1), ("fd", 0), ("gp", 1), ("fu", 2), ("fd", 1),
                          ("gp", 2), ("fu", 3), ("fd", 2), ("gp", 3), ("fd", 3), ("fu", 4), ("fu", 5), ("fd", 4),
                          ("fu", 6), ("fd", 5), ("fu", 7), ("fd", 6)]
            elif last:
                heavy += [("gp", 0), ("gp", 1), ("gp", 2), ("gp", 3)]
                for k in range(4):
                    heavy += [("oh", ot[k], 0), ("oh", ot[k], 1)]
                for k in range(4):
                    heavy += [("oh", at[k], 0), ("oh", at[k], 1)]
            elif has_o:
                for k in range(4):
                    heavy += [("oh", ot[k], 0), ("oh", ot[k], 1), ("gp", k)]
            else:
                heavy += [("gp", 0), ("gp", 1), ("gp", 2), ("gp", 3)]
            pm_ready = []
            pmm_done[G] = 0

            def ready(h):
                if h[0] != "oh":
                    return True
                ti = h[1]
                if ti // 4 == G:
                    return ti in atr_done and pmm_done[G] == 4
                return True

            def take(do_tick=False):
                if do_tick:
                    tick()
                if not heavy or not ready(heavy[0]):
                    return
                h = heavy.pop(0)
                if h[0] == "oh":
                    O_half(h[1], h[2])
                    if h[1] == NT - 1:
                        O_last_stat(h[1], h[2])
                        if h[2] == 1:
                            O_last_a(h[1])
                            later(1, lambda: O_last_b(NT - 1))
                    elif h[2] == 1:
                        otail(h[1], delay=1)
                elif h[0] == "ga":
                    M_chunk(G, h[1])
                elif h[0] == "fu":
                    F_u(h[1])
                elif h[0] == "fd":
                    F_D(h[1])
                    if h[1] == 3:
                        fd3["done"] = True
                        pm_ready.extend(gp_wait)
                        del gp_wait[:]
                else:
                    M_chunk(G, 4 + h[1])
                    (pm_ready if fd3["done"] else gp_wait).append(h[1])

            def pmm():
                while pm_ready:
                    P_mm(G, pm_ready.pop(0))
                    pmm_done[G] += 1

            for i in range(4):
                nxt = ft[i] + 1
                if has_f and nxt < NT:
                    pre(nxt, delay=1)
                A_qk(at[i])
                take()
                if has_f and G > 0:
                    F_u(ft[i])
                A_pv(at[i], 0)
                take()
                A_pv(at[i], 1)
                if i == 0:
                    take()
                    take()
                else:
                    take()
                A_tr(at[i])
                atr_done.add(at[i])
                tick()
                if has_f:
                    F_mm(ft[i])
                take()
                if i == 0:
                    F_tr2(at[3])
                    if G > 0:
                        F_D(at[3])
                elif has_f:
                    F_tr2(ft[i - 1])
                    if G > 0:
                        F_D(ft[i - 1])
                take(True)
                if has_f and nxt < NT:
                    F_tr(nxt)
                take(True)
                pmm()
                if G == 0:
                    dma_wout(2 * i, 2 * i + 2)
                if G == 0 and i == 0:
                    dma("sp", "gf", gfr[:], gf_d, writes=["gfr"])
            guard = 0
            while heavy and guard < 100:
                take(True)
                guard += 1
            assert not heavy
            pmm()
            flush()
        S.add("sp", lambda e: e.nop(), reads=[("out", ti) for ti in range(NT)] + [("outh", NT - 1)])

        S.emit(nc, es)
    return nc


_NC_CACHE = {}


def _host_tables(seg):
    half = 32
    inv_freq = np.array([
        0x3f800000, 0x3f3ff911, 0x3f0ff59a, 0x3ed7e89b, 0x3ea1e89b, 0x3e72d423, 0x3e361887, 0x3e088d77,
        0x3dcccccd, 0x3d99940d, 0x3d6655c2, 0x3d2cba15, 0x3d0186e3, 0x3cc2434f, 0x3c91ad39, 0x3c5a7bf2,
        0x3c23d70a, 0x3bf5b9b0, 0x3bb8449c, 0x3b8a2e77, 0x3b4f3e38, 0x3b1b690d, 0x3ae91528, 0x3aaec98e,
        0x3a83126f, 0x3a44948c, 0x3a136a16, 0x39dd1725, 0x39a5cb60, 0x3978a815, 0x393a7753, 0x390bd472,
    ], dtype=np.uint32).view(np.float32)
    pos = (seg * T - 128 + np.arange((NT + 1) * 128)).astype(np.float32)
    ang = (pos[:, None] * inv_freq[None, :]).astype(np.float32)
    cos_t = np.cos(ang).astype(np.float32)
    sin_t = np.sin(ang).astype(np.float32)
    cos_t = np.ascontiguousarray(cos_t.reshape(NT + 1, 128, 32).transpose(1, 0, 2).reshape(128, (NT + 1) * 32))
    sin_t = np.ascontiguousarray(sin_t.reshape(NT + 1, 128, 32).transpose(1, 0, 2).reshape(128, (NT + 1) * 32))
    j = np.arange(128)[:, None]
    i = np.arange(128)[None, :]
    NEG = np.float32(-30000.0)
    m_cur = np.where(j <= i, np.float32(0.0), NEG).astype(np.float32)
    m_prev = np.where(j > i, np.float32(0.0), NEG).astype(np.float32)
    m_prev0 = m_prev if seg != 0 else np.full((128, 128), NEG, np.float32)
    masks = np.stack([m_prev0, m_prev, m_cur], axis=1).astype(np.float32)
    atab = np.empty((128, 3, 2, 128), np.float32)
    for p in range(128):
        for h in range(2):
            atab[p, :, h, :] = masks[:, :, (p % 64) + 64 * h].T
    btab = (np.arange(128)[:, None] % 64 == np.arange(64)[None, :]).astype(np.float32)
    tp = np.arange(128)[:, None]
    tq = np.arange(128)[None, :]
    band = np.zeros((128, 12, 128), np.float32)
    for g, w in enumerate(WINDOWS):
        inwin = ((tq - tp) >= 0) & ((tq - tp) < w)
        cur = np.where(inwin, np.float32(1.0 / w), np.float32(0.0)) - (tp == tq).astype(np.float32)
        prev = np.where((tq + 128 - tp) < w, np.float32(1.0 / w), np.float32(0.0))
        band[:, 2 * g, :] = cur
        band[:, 2 * g + 1, :] = prev
        if seg == 0:
            cnt = np.minimum(tq + 1, w).astype(np.float32)
            cur0 = np.where(inwin, np.float32(1.0) / cnt, np.float32(0.0)) - (tp == tq).astype(np.float32)
        else:
            cur0 = cur
        band[:, 8 + g, :] = cur0
    return cos_t, sin_t, (atab, btab), band


def kernel(x, norm_in, w_in, attn_sinks, w_pool, pool_scale, w_out, norm_final):
    x = np.asarray(x, np.float32)
    w_in0 = np.asarray(w_in, np.float32)[0]
    qperm = []
    for j in range(4):
        for kv in range(2):
            h = kv * 4 + j
            qperm.extend(range(h * 64, (h + 1) * 64))
    cols = np.concatenate([np.array(qperm), np.arange(512, 768), np.arange(1280, 1792), np.arange(768, 1280),
                           np.arange(1792, DIN)])
    w_in_p = np.ascontiguousarray(w_in0[:, cols])
    w_out0 = np.ascontiguousarray(np.asarray(w_out, np.float32)[0])
    w_pool0 = np.ascontiguousarray(np.asarray(w_pool, np.float32)[0])
    g_row = np.ascontiguousarray(np.asarray(norm_in, np.float32)[0][None, :])
    gf_row = np.ascontiguousarray(np.asarray(norm_final, np.float32)[None, :])
    sink_rep = np.ascontiguousarray(np.broadcast_to(np.asarray(attn_sinks, np.float32)[0][None, :], (128, 8)))
    pscale = np.ascontiguousarray(np.asarray(pool_scale, np.float32)[0].reshape(4, 128).T)
    ident = np.eye(128, dtype=np.float32)

    in_maps = []
    for c in range(N_CORES):
        b, seg = c // 4, c % 4
        xc = np.ascontiguousarray(x[b, seg * T:(seg + 1) * T, :])
        if seg == 0:
            xh = np.zeros((128, D), np.float32)
        else:
            xh = np.ascontiguousarray(x[b, seg * T - 128:seg * T, :])
        cos_t, sin_t, masks, band = _host_tables(seg)
        in_maps.append({
            "x": xc, "xh": xh, "w_in": w_in_p, "w_out": w_out0, "w_pool": w_pool0,
            "g_row": g_row, "gf_row": gf_row, "sink_rep": sink_rep, "pscale": pscale,
            "cos_t": cos_t, "sin_t": sin_t, "atab": masks[0], "btab": masks[1], "band": band, "ident": ident,
        })
    if "nc" not in _NC_CACHE:
        _NC_CACHE["nc"] = build_nc()
    nc = _NC_CACHE["nc"]
    res = run_bass_kernel_spmd(nc, in_maps, core_ids=list(range(N_CORES)))
    out = np.empty((BATCH, SEQ, D), np.float32)
    for c in range(N_CORES):
        b, seg = c // 4, c % 4
        out[b, seg * T:(seg + 1) * T, :] = res.results[c]["out"]
    return out
```
